# Optimizing a Trainium2 kernel written in Bass

```python
import jax, jax.numpy as jnp
from jax import lax
import numpy as np

D_MODEL = 1024
BATCH = 8
SEQ = 2048
DEPTH = 4

MEM_LEN = 256
N_MIXERS = 2
MIX_WIDTH = 3 * D_MODEL // 2
XA_HEADS = 4
XA_HEAD_DIM = D_MODEL // 8
XA_WIDTH = XA_HEADS * XA_HEAD_DIM
IN_WIDTH = 2 * MIX_WIDTH + XA_WIDTH
OUT_WIDTH = MIX_WIDTH + XA_WIDTH
CONV_WIDTH = 4
LRU_BLOCKS = 8
LRU_BS = MIX_WIDTH // LRU_BLOCKS
LRU_C = 8.0
ML_HEADS = 4
ML_HEAD_DIM = MIX_WIDTH // ML_HEADS
ML_QKV_BS = 4
ML_QKV_BLOCKS = MIX_WIDTH // ML_QKV_BS
ML_CHUNK = 64
D_FF = ((8 * D_MODEL // 3 + 127) // 128) * 128
N_A = (DEPTH + 1) // 2
N_B = DEPTH // 2
EPS = 1e-6

kernel_name = 'hybrid_rglru_mlstm_memxattn_macaron'

F32 = jnp.float32


def _rmsnorm(x, g):
    xf = x.astype(F32)
    y = xf * lax.rsqrt(jnp.mean(xf * xf, axis=-1, keepdims=True) + EPS)
    return (y * g.astype(F32)).astype(x.dtype)


def _swiglu(x, w_gate, w_up, w_down):
    return (jax.nn.silu(x @ w_gate) * (x @ w_up)) @ w_down


def _causal_dwconv(x, w, b):
    k_w, s = w.shape[0], x.shape[1]
    xp = jnp.pad(x, ((0, 0), (k_w - 1, 0), (0, 0)))
    y = b + xp[:, 0:s] * w[0]
    for j in range(1, k_w):
        y = y + xp[:, j:j + s] * w[j]
    return y


def _block_diag(x, w):
    b, s, _ = x.shape
    n, bi, bo = w.shape
    y = jnp.einsum('bsni,nio->bsno', x.reshape(b, s, n, bi), w)
    return y.reshape(b, s, n * bo)


def _rg_lru(x, w_r, b_r, w_i, b_i, lam):
    xf = x.astype(F32)
    r = jax.nn.sigmoid((_block_diag(x, w_r) + b_r).astype(F32))
    gi = jax.nn.sigmoid((_block_diag(x, w_i) + b_i).astype(F32))
    log_a = -LRU_C * r * jax.nn.softplus(-lam.astype(F32))
    a = jnp.exp(log_a)
    u = jnp.sqrt(-jnp.expm1(2.0 * log_a)) * (gi * xf)

    def combine(e1, e2):
        a1, b1 = e1
        a2, b2 = e2
        return a1 * a2, a2 * b1 + b2

    _, h = lax.associative_scan(combine, (a, u), axis=1)
    return h.astype(x.dtype)


def _mlstm_chunkwise(q, k, v, log_i, log_f):
    b, s, h, dh = q.shape
    nc = s // ML_CHUNK

    def to_chunks(t):
        return jnp.moveaxis(t.reshape(b, nc, ML_CHUNK, h, dh), (1, 3), (0, 2))

    def gate_chunks(t):
        return t.reshape(b, nc, ML_CHUNK, h).transpose(1, 0, 3, 2)

    causal = jnp.tril(jnp.ones((ML_CHUNK, ML_CHUNK), dtype=bool))

    def step(carry, inp):
        c_s, n_s, m = carry
        qc, kc, vc, ic, fc = inp
        bcum = jnp.cumsum(fc, axis=-1)
        dmat = bcum[..., :, None] - bcum[..., None, :] + ic[..., None, :]
        dmat = jnp.where(causal, dmat, -jnp.inf)
        inter = bcum + m[..., None]
        m_t = jnp.maximum(inter, jnp.max(dmat, axis=-1))
        w_intra = jnp.exp(dmat - m_t[..., None])
        w_inter = jnp.exp(inter - m_t)
        sc = jnp.einsum('bhtd,bhsd->bhts', qc, kc) * w_intra
        num = jnp.einsum('bhts,bhsd->bhtd', sc, vc) + w_inter[..., None] * jnp.einsum('bhvk,bhtk->bhtv', c_s, qc)
        den = jnp.sum(sc, axis=-1) + w_inter * jnp.einsum('bhk,bhtk->bht', n_s, qc)
        h_out = num / jnp.maximum(jnp.abs(den), jnp.exp(-m_t))[..., None]
        b_last = bcum[..., -1]
        g = b_last[..., None] - bcum + ic
        m_new = jnp.maximum(b_last + m, jnp.max(g, axis=-1))
        w_s = jnp.exp(g - m_new[..., None])
        decay = jnp.exp(b_last + m - m_new)
        c_new = decay[..., None, None] * c_s + jnp.einsum('bhsv,bhsk->bhvk', vc * w_s[..., None], kc)
        n_new = decay[..., None] * n_s + jnp.einsum('bhs,bhsk->bhk', w_s, kc)
        return (c_new, n_new, m_new), h_out

    init = (jnp.zeros((b, h, dh, dh), F32), jnp.zeros((b, h, dh), F32), jnp.zeros((b, h), F32))
    _, hs = lax.scan(step, init, (to_chunks(q), to_chunks(k), to_chunks(v), gate_chunks(log_i), gate_chunks(log_f)))
    return hs.transpose(1, 0, 3, 2, 4).reshape(b, s, h, dh)


def _lru_mixer(main, gate, conv_w, conv_b, w_r, b_r, w_i, b_i, lam):
    xc = _causal_dwconv(main, conv_w, conv_b)
    return _rg_lru(xc, w_r, b_r, w_i, b_i, lam) * jax.nn.gelu(gate, approximate=True)


def _mlstm_mixer(main, gate, conv_w, conv_b, w_q, w_k, w_v, w_gates, b_gates, ln_g, skip):
    b, s, _ = main.shape
    xc = jax.nn.silu(_causal_dwconv(main, conv_w, conv_b))
    q = _block_diag(xc, w_q)
    k = _block_diag(xc, w_k)
    v = _block_diag(main, w_v)
    g = (jnp.concatenate([q, k, v], axis=-1) @ w_gates + b_gates).astype(F32)
    log_i = g[..., :ML_HEADS]
    log_f = jax.nn.log_sigmoid(g[..., ML_HEADS:])

    def heads(t):
        return t.astype(F32).reshape(b, s, ML_HEADS, ML_HEAD_DIM)

    h = _mlstm_chunkwise(heads(q), heads(k) * (ML_HEAD_DIM ** -0.5), heads(v), log_i, log_f)
    mu = jnp.mean(h, axis=-1, keepdims=True)
    var = jnp.mean(jnp.square(h - mu), axis=-1, keepdims=True)
    h = ((h - mu) * lax.rsqrt(var + EPS)).reshape(b, s, MIX_WIDTH) * ln_g.astype(F32)
    h = h.astype(main.dtype)
    return (h + skip * xc) * jax.nn.silu(gate)


def _cross_attn(qx, mem_n, w_mem_kv):
    b, s, _ = qx.shape
    k_m, v_m = jnp.split(mem_n @ w_mem_kv, 2, axis=-1)
    q = qx.reshape(b, s, XA_HEADS, XA_HEAD_DIM).astype(F32)
    k_m = k_m.reshape(b, -1, XA_HEADS, XA_HEAD_DIM).astype(F32)
    v_m = v_m.reshape(b, -1, XA_HEADS, XA_HEAD_DIM).astype(F32)
    sc = jnp.einsum('bshd,bmhd->bhsm', q, k_m) * (XA_HEAD_DIM ** -0.5)
    p = jax.nn.softmax(sc, axis=-1)
    o = jnp.einsum('bhsm,bmhd->bshd', p, v_m)
    return o.reshape(b, s, XA_WIDTH).astype(qx.dtype)


def setup_inputs(seed: int = 0) -> dict:
    key = jax.random.key(seed)
    it = iter(jax.random.split(key, 40))

    def normal(shape, fan_in):
        return jax.random.normal(next(it), shape, F32) * (fan_in ** -0.5)

    def gain(shape):
        return 1.0 + 0.02 * jax.random.normal(next(it), shape, F32)

    def bias(shape, scale=0.02):
        return scale * jax.random.normal(next(it), shape, F32)

    u = jax.random.uniform(next(it), (N_A, MIX_WIDTH), F32, minval=0.9, maxval=0.999)
    a0 = u ** (1.0 / LRU_C)
    lru_lambda = jnp.log(a0) - jnp.log1p(-a0)
    b_ig = bias((N_B, ML_HEADS), 0.1)
    b_fg = jnp.linspace(3.0, 6.0, ML_HEADS, dtype=F32)[None, :] + bias((N_B, ML_HEADS), 0.01)

    return {
        'x': normal((BATCH, SEQ, D_MODEL), 1),
        'mem': normal((BATCH, MEM_LEN, D_MODEL), 1),
        'ffn1_norm': gain((DEPTH, D_MODEL)),
        'ffn1_w_gate': normal((DEPTH, D_MODEL, D_FF), D_MODEL),
        'ffn1_w_up': normal((DEPTH, D_MODEL, D_FF), D_MODEL),
        'ffn1_w_down': normal((DEPTH, D_FF, D_MODEL), D_FF),
        'mix_norm': gain((DEPTH, D_MODEL)),
        'w_in': normal((DEPTH, D_MODEL, IN_WIDTH), D_MODEL),
        'mem_norm': gain((DEPTH, D_MODEL)),
        'w_mem_kv': normal((DEPTH, D_MODEL, 2 * XA_WIDTH), D_MODEL),
        'w_out': normal((DEPTH, OUT_WIDTH, D_MODEL), OUT_WIDTH),
        'ffn2_norm': gain((DEPTH, D_MODEL)),
        'ffn2_w_gate': normal((DEPTH, D_MODEL, D_FF), D_MODEL),
        'ffn2_w_up': normal((DEPTH, D_MODEL, D_FF), D_MODEL),
        'ffn2_w_down': normal((DEPTH, D_FF, D_MODEL), D_FF),
        'lru_conv_w': normal((N_A, CONV_WIDTH, MIX_WIDTH), CONV_WIDTH),
        'lru_conv_b': bias((N_A, MIX_WIDTH)),
        'lru_w_r': normal((N_A, LRU_BLOCKS, LRU_BS, LRU_BS), LRU_BS),
        'lru_b_r': bias((N_A, MIX_WIDTH)),
        'lru_w_i': normal((N_A, LRU_BLOCKS, LRU_BS, LRU_BS), LRU_BS),
        'lru_b_i': bias((N_A, MIX_WIDTH)),
        'lru_lambda': lru_lambda,
        'ml_conv_w': normal((N_B, CONV_WIDTH, MIX_WIDTH), CONV_WIDTH),
        'ml_conv_b': bias((N_B, MIX_WIDTH)),
        'ml_w_q': normal((N_B, ML_QKV_BLOCKS, ML_QKV_BS, ML_QKV_BS), ML_QKV_BS),
        'ml_w_k': normal((N_B, ML_QKV_BLOCKS, ML_QKV_BS, ML_QKV_BS), ML_QKV_BS),
        'ml_w_v': normal((N_B, ML_QKV_BLOCKS, ML_QKV_BS, ML_QKV_BS), ML_QKV_BS),
        'ml_w_gates': normal((N_B, 3 * MIX_WIDTH, 2 * ML_HEADS), 3 * MIX_WIDTH),
        'ml_b_gates': jnp.concatenate([b_ig, b_fg], axis=-1),
        'ml_ln_g': gain((N_B, MIX_WIDTH)),
        'ml_skip': gain((N_B, MIX_WIDTH)),
        'final_norm': gain((D_MODEL,)),
    }


def reference(x, mem, ffn1_norm, ffn1_w_gate, ffn1_w_up, ffn1_w_down, mix_norm, w_in, mem_norm, w_mem_kv, w_out,
              ffn2_norm, ffn2_w_gate, ffn2_w_up, ffn2_w_down, lru_conv_w, lru_conv_b, lru_w_r, lru_b_r, lru_w_i,
              lru_b_i, lru_lambda, ml_conv_w, ml_conv_b, ml_w_q, ml_w_k, ml_w_v, ml_w_gates, ml_b_gates, ml_ln_g,
              ml_skip, final_norm):
    h = x
    for l in range(DEPTH):
        h = h + 0.5 * _swiglu(_rmsnorm(h, ffn1_norm[l]), ffn1_w_gate[l], ffn1_w_up[l], ffn1_w_down[l])
        u = _rmsnorm(h, mix_norm[l]) @ w_in[l]
        main = u[..., :MIX_WIDTH]
        gate = u[..., MIX_WIDTH:2 * MIX_WIDTH]
        qx = u[..., 2 * MIX_WIDTH:]
        j = l // N_MIXERS
        if l % N_MIXERS == 0:
            y = _lru_mixer(main, gate, lru_conv_w[j], lru_conv_b[j], lru_w_r[j], lru_b_r[j], lru_w_i[j],
                           lru_b_i[j], lru_lambda[j])
        else:
            y = _mlstm_mixer(main, gate, ml_conv_w[j], ml_conv_b[j], ml_w_q[j], ml_w_k[j], ml_w_v[j],
                             ml_w_gates[j], ml_b_gates[j], ml_ln_g[j], ml_skip[j])
        xa = _cross_attn(qx, _rmsnorm(mem, mem_norm[l]), w_mem_kv[l])
        h = h + jnp.concatenate([y, xa], axis=-1) @ w_out[l]
        h = h + 0.5 * _swiglu(_rmsnorm(h, ffn2_norm[l]), ffn2_w_gate[l], ffn2_w_up[l], ffn2_w_down[l])
    return _rmsnorm(h, final_norm)
```

```python
import math
from contextlib import ExitStack

import numpy as np
import concourse.bass as bass
import concourse.mybir as mybir
from concourse.bass_utils import run_bass_kernel_spmd

F32 = mybir.dt.float32
BF16 = mybir.dt.bfloat16
AF = mybir.ActivationFunctionType
ALU = mybir.AluOpType

D = 1024
S = 2048
TH = 1024
NT = 2
FC = 22
DEPTH = 4
EPS = 1e-6
LN_SCALE = math.log(384.0 ** -0.5)
XA_SCALE = 128.0 ** -0.5
NCORES = 8

G_FFN1, G_MIX, G_FFN2, G_MEM = 0, 8, 16, 24


class Op:
    __slots__ = ("eng", "fn", "reads", "writes", "dma", "waits", "inc", "seq", "cum", "barrier")

    def __init__(self, eng, fn, reads, writes, dma, barrier=False):
        self.eng = eng
        self.fn = fn
        self.reads = tuple(reads)
        self.writes = tuple(writes)
        self.dma = dma
        self.waits = []
        self.inc = False
        self.seq = 0
        self.cum = 0
        self.barrier = barrier


class Prog:
    ENGS = ("pe", "act", "dve", "pool", "sp")

    def __init__(self):
        self.ops = []

    def add(self, eng, fn, reads=(), writes=(), dma=None):
        self.ops.append(Op(eng, fn, reads, writes, dma))

    def barrier(self, fn):
        self.ops.append(Op("pool", fn, (), (), None, barrier=True))

    def resolve(self):
        last_w = {}
        readers = {}
        known = {e: {} for e in self.ENGS}
        dma_cum = {}
        bar_idx = None
        for i, op in enumerate(self.ops):
            deps = set()
            if op.barrier:
                for r in list(last_w.keys()):
                    if r.startswith("A:"):
                        deps.add(last_w.pop(r))
                for r in list(readers.keys()):
                    if r.startswith("A:"):
                        deps.update(readers.pop(r))
                if bar_idx is not None:
                    deps.add(bar_idx)
                bar_idx = i
            else:
                arena = False
                for r in op.reads:
                    if r.startswith("A:"):
                        arena = True
                    if r in last_w:
                        deps.add(last_w[r])
                for w in op.writes:
                    if w.startswith("A:"):
                        arena = True
                    if w in last_w:
                        deps.add(last_w[w])
                    deps.update(readers.get(w, ()))
                if arena and bar_idx is not None:
                    deps.add(bar_idx)
            deps.discard(i)
            need = {}
            for j in deps:
                pj = self.ops[j]
                if pj.dma is not None:
                    key = ("dma", pj.dma)
                    val = pj.cum
                else:
                    if pj.eng == "pe" and op.eng == "pe" and op.dma is None:
                        continue
                    key = ("eng", pj.eng)
                    val = j
                if need.get(key, -1) < val:
                    need[key] = val
            kn = known[op.eng]
            for key, val in need.items():
                if kn.get(key, -1) >= val:
                    continue
                kn[key] = val
                if key[0] == "eng":
                    self.ops[val].inc = True
                op.waits.append((key, val))
            if op.dma is not None:
                dma_cum[op.dma] = dma_cum.get(op.dma, 0) + 16
                op.cum = dma_cum[op.dma]
            for r in op.reads:
                readers.setdefault(r, []).append(i)
            for w in op.writes:
                last_w[w] = i
                readers[w] = []
        cnt = {e: 0 for e in self.ENGS}
        for op in self.ops:
            if op.dma is None and op.inc:
                cnt[op.eng] += 1
                op.seq = cnt[op.eng]
        self.dma_tot = dma_cum
        return cnt

    def emit(self, nc, final_waits=()):
        cnt = self.resolve()
        per = {e: [] for e in self.ENGS}
        for op in self.ops:
            per[op.eng].append(op)
        with ExitStack() as st:
            esem = {e: st.enter_context(nc.semaphore("s_" + e)) for e in self.ENGS}
            dsem = {d: st.enter_context(nc.semaphore("d_" + d)) for d in sorted(self.dma_tot)}
            block = st.enter_context(nc.Block())

            def run(eng_name):
                def body(eng):
                    for op in per[eng_name]:
                        for key, val in op.waits:
                            if key[0] == "eng":
                                eng.wait_ge(esem[key[1]], self.ops[val].seq)
                            else:
                                eng.wait_ge(dsem[key[1]], val)
                        ins = op.fn(eng)
                        if op.dma is not None:
                            ins.then_inc(dsem[op.dma], 16)
                        elif op.inc:
                            ins.then_inc(esem[eng_name], 1)
                    if eng_name == "sp":
                        for d in final_waits:
                            eng.wait_ge(dsem[d], self.dma_tot[d])

                return body

            block.tensor(run("pe"))
            block.scalar(run("act"))
            block.vector(run("dve"))
            block.gpsimd(run("pool"))
            block.sync(run("sp"))
        return cnt


def _chunk_rows(w, kc):
    return np.ascontiguousarray(w.reshape(kc, 128, -1).transpose(1, 0, 2))


def prep_shared(inp):
    f = lambda a: np.asarray(a, dtype=np.float32)
    sh = {}
    gall = np.zeros((128, 136), np.float32)
    for l in range(DEPTH):
        for off, key in ((G_FFN1, "ffn1_norm"), (G_MIX, "mix_norm"), (G_FFN2, "ffn2_norm"), (G_MEM, "mem_norm")):
            gall[:, l * 32 + off:l * 32 + off + 8] = f(inp[key])[l].reshape(8, 128).T
    gall[:, 128:136] = f(inp["final_norm"]).reshape(8, 128).T
    sh["gall"] = gall
    wgu = np.empty((DEPTH, 2, FC, 128, 2, 8, 128), np.float32)
    wd = np.empty((DEPTH, 2, 8, 128, FC, 128), np.float32)
    for fi, pre in enumerate(("ffn1", "ffn2")):
        g = f(inp[pre + "_w_gate"]).reshape(DEPTH, 8, 128, FC, 128)
        u = f(inp[pre + "_w_up"]).reshape(DEPTH, 8, 128, FC, 128)
        wgu[:, fi, :, :, 0] = g.transpose(0, 3, 2, 1, 4)
        wgu[:, fi, :, :, 1] = u.transpose(0, 3, 2, 1, 4)
        dn = f(inp[pre + "_w_down"]).reshape(DEPTH, FC, 128, 8, 128)
        wd[:, fi] = dn.transpose(0, 3, 2, 1, 4)
    sh["wgu"] = wgu
    sh["wd"] = wd
    w_in = f(inp["w_in"]).reshape(DEPTH, 8, 128, 3584).transpose(0, 2, 1, 3)
    win_m = np.empty((DEPTH, 4, 128, 8, 384), np.float32)
    win_g = np.empty((DEPTH, 4, 128, 8, 384), np.float32)
    for g in range(4):
        win_m[:, g] = w_in[..., 384 * g:384 * g + 384]
        win_g[:, g] = w_in[..., 1536 + 384 * g:1536 + 384 * g + 384]
    sh["win_m"] = win_m
    sh["win_g"] = win_g
    sh["win_q"] = np.ascontiguousarray(w_in[..., 3072:3584])
    wkv = f(inp["w_mem_kv"]).reshape(DEPTH, 8, 128, 1024).transpose(0, 2, 1, 3)
    sh["wkv_k"] = np.ascontiguousarray(wkv[..., 0:512])
    sh["wkv_v"] = np.ascontiguousarray(wkv[..., 512:1024])
    sh["wo"] = np.ascontiguousarray(f(inp["w_out"]).reshape(DEPTH, 16, 128, 1024).transpose(0, 2, 1, 3))
    def chan(v):
        return v.reshape(12, 128).T
    lruv = np.zeros((2, 128, 12, 8), np.float32)
    mlv = np.zeros((2, 128, 12, 8), np.float32)
    for j in range(2):
        for t in range(4):
            lruv[j, :, :, t] = chan(f(inp["lru_conv_w"])[j, t])
            mlv[j, :, :, t] = chan(f(inp["ml_conv_w"])[j, t])
        lruv[j, :, :, 4] = chan(f(inp["lru_conv_b"])[j])
        lruv[j, :, :, 5] = chan(f(inp["lru_b_r"])[j])
        lruv[j, :, :, 6] = chan(f(inp["lru_b_i"])[j])
        lruv[j, :, :, 7] = chan(f(inp["lru_lambda"])[j])
        mlv[j, :, :, 4] = chan(f(inp["ml_conv_b"])[j])
        mlv[j, :, :, 5] = chan(f(inp["ml_ln_g"])[j])
        mlv[j, :, :, 6] = chan(f(inp["ml_skip"])[j])
    sh["lruv"] = lruv
    sh["mlv"] = mlv
    lrw = np.zeros((2, 4, 128, 2, 3, 384), np.float32)
    for ri, key in enumerate(("lru_w_r", "lru_w_i")):
        w = f(inp[key])
        for j in range(2):
            for n in range(8):
                g = n // 2
                dense = np.zeros((384, 384), np.float32)
                o = (n % 2) * 192
                dense[o:o + 192, o:o + 192] = w[j, n]
                if n % 2 == 1:
                    dense[0:192, 0:192] = w[j, n - 1]
                    lrw[j, g, :, ri] = dense.reshape(3, 128, 384).transpose(1, 0, 2)
    sh["lrw"] = lrw
    mlbd = np.zeros((2, 128, 3, 12, 128), np.float32)
    mlbdT = np.zeros((2, 128, 3, 12, 128), np.float32)
    for qi, key in enumerate(("ml_w_q", "ml_w_k", "ml_w_v")):
        w = f(inp[key])
        for j in range(2):
            bd = np.zeros((12, 128, 128), np.float32)
            wj = w[j].reshape(12, 32, 4, 4)
            for b in range(32):
                bd[:, 4 * b:4 * b + 4, 4 * b:4 * b + 4] = wj[:, b]
            mlbd[j, :, qi] = bd.transpose(1, 0, 2)
            mlbdT[j, :, qi] = bd.transpose(2, 0, 1)
    sh["mlbd"] = mlbd
    sh["mlbdT"] = mlbdT
    sh["mlwg"] = np.ascontiguousarray(f(inp["ml_w_gates"]).reshape(2, 36, 128, 8).transpose(0, 2, 1, 3))
    bg = np.zeros((8, 2), np.float32)
    bg[:, :] = f(inp["ml_b_gates"]).T
    sh["mlbg"] = bg
    cst = np.zeros((128, 4, 128), np.float32)
    cst[:, 0] = 1.0
    cst[:, 1] = np.eye(128, dtype=np.float32)
    cst[:, 2] = np.triu(np.ones((128, 128), np.float32))
    cst[:, 3] = np.where(np.triu(np.ones((128, 128))) > 0, 0.0, -32768.0)
    sh["cst"] = cst
    return sh


def prep_core(x_b, mem_b):
    xin = np.ascontiguousarray(np.asarray(x_b, np.float32).T.reshape(8, 128, S).transpose(1, 0, 2))
    memT = np.ascontiguousarray(np.asarray(mem_b, np.float32).T.reshape(8, 128, 256).transpose(1, 0, 2))
    return xin, memT


class Ring:
    def __init__(self, bld, name, nslots, shape, dtype, arena=False):
        self.b = bld
        self.name = name
        self.n = nslots
        self.arena = arena
        self.tiles = [bld.tile(f"{name}{i}", shape, dtype, arena) for i in range(nslots)]
        self.res = [("A:" if arena else "") + f"{name}{i}" for i in range(nslots)]
        self.cnt = 0

    def load(self, fn_src):
        s = self.cnt % self.n
        self.cnt += 1
        t = self.tiles[s]
        out_ap, in_ap = fn_src(t)
        self.b.p.add("pool", lambda e, o=out_ap, i=in_ap: e.dma_start(out=o, in_=i),
                     writes=[self.res[s]], dma=f"{self.name}{s}")
        return s


class Builder:
    ARENA_BYTES = 128 * 1024

    def __init__(self, layers, first, last):
        self.layers = layers
        self.first = first
        self.last = last
        self.nc = bass.Bass("TRN2", target_bir_lowering=False)
        self.p = Prog()
        self.st = ExitStack()
        self.ar_off = 0
        self.ar_peak = 0
        self.ps_rr = 0
        self.ps_reserved = set()
        self.uid = 0

    def dram(self, name, shape, dtype=F32, kind="ExternalInput"):
        return self.nc.dram_tensor(name, list(shape), dtype, kind=kind).ap()

    def sb(self, name, shape, dtype):
        return self.st.enter_context(self.nc.sbuf_tensor(name, list(shape), dtype))

    def tile(self, name, shape, dtype, arena):
        if not arena:
            return self.sb(name, shape, dtype)
        return self.aalloc(shape, dtype)

    def aalloc(self, shape, dtype):
        esz = 4 if dtype == F32 else 2
        n = 1
        for s_ in shape[1:]:
            n *= s_
        nbytes = (n * esz + 63) // 64 * 64
        off = self.ar_off
        self.ar_off += nbytes
        self.ar_peak = max(self.ar_peak, self.ar_off)
        assert self.ar_off <= self.ARENA_BYTES, f"arena overflow {self.ar_off}"
        raw = self.arena[0:shape[0], off // 4:(off + nbytes) // 4]
        if dtype != F32:
            raw = raw.bitcast(dtype)
        raw = raw[:, 0:n]
        if len(shape) == 2:
            return raw
        if len(shape) == 3:
            return raw.rearrange("p (a b) -> p a b", b=shape[2])
        if len(shape) == 4:
            return raw.rearrange("p (a b c) -> p a b c", b=shape[2], c=shape[3])
        raise ValueError(shape)

    def phase(self):
        self.ar_off = 0
        scr = self.bar_scr
        self.p.barrier(lambda e: e.memset(scr[:, 0:8], 0.0))

    def ps_next(self):
        for _ in range(16):
            i = self.ps_rr % 8
            self.ps_rr += 1
            if i not in self.ps_reserved:
                return i
        raise RuntimeError("no psum")

    @staticmethod
    def PS(i, qs=(0, 1, 2, 3)):
        return [f"ps{i}q{q}" for q in qs]

    def mm(self, out, lhsT, rhs, start, stop, reads, writes):
        self.p.add("pe", lambda e: e.matmul(out, lhsT=lhsT, rhs=rhs, start=start, stop=stop),
                   reads=reads, writes=writes)

    def act(self, out, in_, func, reads, writes, bias=None, scale=None):
        kw = {}
        if bias is not None:
            kw["bias"] = bias
        if scale is not None:
            kw["scale"] = scale
        self.p.add("act", lambda e: e.activation(out=out, in_=in_, func=func, **kw), reads=reads, writes=writes)

    def tt(self, eng, out, in0, in1, op, reads, writes):
        self.p.add(eng, lambda e: e.tensor_tensor(out=out, in0=in0, in1=in1, op=op), reads=reads, writes=writes)

    def stt(self, out, in0, scalar, in1, op0, op1, reads, writes):
        self.p.add("dve", lambda e: e.scalar_tensor_tensor(out=out, in0=in0, scalar=scalar, in1=in1, op0=op0, op1=op1),
                   reads=reads, writes=writes)

    def ts(self, eng, out, in0, s1, s2, op0, op1, reads, writes):
        if s2 is None:
            self.p.add(eng, lambda e: e.tensor_scalar(out=out, in0=in0, scalar1=s1, scalar2=None, op0=op0),
                       reads=reads, writes=writes)
        else:
            self.p.add(eng, lambda e: e.tensor_scalar(out=out, in0=in0, scalar1=s1, scalar2=s2, op0=op0, op1=op1),
                       reads=reads, writes=writes)

    def copy(self, eng, out, in_, reads, writes):
        if eng == "act":
            self.act(out, in_, AF.Copy, reads, writes)
        else:
            self.p.add(eng, lambda e: e.tensor_copy(out=out, in_=in_), reads=reads, writes=writes)

    def recip(self, out, in_, reads, writes):
        self.p.add("dve", lambda e: e.reciprocal(out=out, in_=in_), reads=reads, writes=writes)

    def dma(self, q, out, in_, reads, writes, sem):
        self.p.add(q, lambda e: e.dma_start(out=out, in_=in_), reads=reads, writes=writes, dma=sem)

    def build(self):
        nc = self.nc
        L = self.layers
        self.d_hin = self.dram("hin", [128, 8, S])
        self.d_mem = self.dram("memT", [128, 8, 256])
        self.d_out = self.dram("hout", [128, 8, S], kind="ExternalOutput")
        names = dict(gall=[128, 136], wgu=[DEPTH, 2, FC, 128, 2, 8, 128], wd=[DEPTH, 2, 8, 128, FC, 128],
                     win_m=[DEPTH, 4, 128, 8, 384], win_g=[DEPTH, 4, 128, 8, 384], win_q=[DEPTH, 128, 8, 512],
                     wkv_k=[DEPTH, 128, 8, 512], wkv_v=[DEPTH, 128, 8, 512], wo=[DEPTH, 128, 16, 1024],
                     lruv=[2, 128, 12, 8], mlv=[2, 128, 12, 8], lrw=[2, 4, 128, 2, 3, 384],
                     mlbd=[2, 128, 3, 12, 128], mlbdT=[2, 128, 3, 12, 128], mlwg=[2, 128, 36, 8], mlbg=[8, 2],
                     cst=[128, 4, 128])
        self.dr = {k: self.dram(k, v) for k, v in names.items()}
        self.scr_xc = self.dram("scr_xc", [12, NT, 128, 512], F32, kind="Internal")
        self.scr_sg = self.dram("scr_sg", [12, NT, 128, 512], F32, kind="Internal")
        self.scr_xcb = self.dram("scr_xcb", [12, NT, 128, 512], BF16, kind="Internal")
        self.scr_mb = self.dram("scr_mb", [12, NT, 128, 512], BF16, kind="Internal")
        self.scr_ct = self.dram("scr_ct", [2, 4, 128, 3, 512], F32, kind="Internal")

        sb = self.sb
        self.hT = sb("hT", [128, 8, TH], F32)
        self.arena = sb("arena", [128, self.ARENA_BYTES // 4], F32)
        self.bar_scr = sb("bar_scr", [128, 8], F32)
        self.gall = sb("gall_sb", [128, 136], F32)
        self.cst = sb("cst_sb", [128, 4, 128], F32)
        self.ones_bf = sb("ones_bf", [128, 128], BF16)
        self.ident_bf = sb("ident_bf", [128, 128], BF16)
        self.mask_bf = sb("mask_bf", [128, 128], BF16)
        self.memT = sb("memT_sb", [128, 8, 256], F32)
        self.lruv = sb("lruv_sb", [128, 2, 12, 8], F32)
        self.mlv = sb("mlv_sb", [128, 2, 12, 8], F32)
        self.mlbg = sb("mlbg_sb", [8, 2], F32)
        self.nsp8 = sb("nsp8", [128, 2, 12], F32)
        self.lru_h = sb("lru_h", [128, 2, 12], F32)
        self.ctail = sb("ctail", [128, DEPTH, 12, 3], F32)
        self.epsc = sb("epsc", [128, 1], F32)
        self.ps = [self.st.enter_context(nc.psum_tensor(f"ps{i}", [128, 512], F32)) for i in range(8)]
        self.wgu = Ring(self, "wgu", 4, [128, 2, 8, 128], BF16)
        self.wdr = Ring(self, "wd", 3, [128, FC, 128], BF16)

        ones_f = self.cst[:, 0, :]
        self.ones_f, self.ident_f, self.tri_f = ones_f, self.cst[:, 1, :], self.cst[:, 2, :]

        self.dma("sp", self.gall[:], self.dr["gall"][:, :], [], ["gall"], "ld0")
        self.dma("sp", self.cst[:], self.dr["cst"][:, :, :], [], ["cst"], "ld1")
        self.dma("sp", self.memT[:], self.d_mem[:, :, :], [], ["memT"], "ld2")
        self.dma("sp", self.lruv[:], self.dr["lruv"].rearrange("j p c t -> p j c t"), [], ["lruv"], "ld3")
        self.dma("sp", self.mlv[:], self.dr["mlv"].rearrange("j p c t -> p j c t"), [], ["mlv"], "ld4")
        self.dma("sp", self.mlbg[:], self.dr["mlbg"][:, :], [], ["mlbg"], "ld5")
        self.copy("dve", self.ones_bf[:], self.cst[:, 0, :], ["cst"], ["ones_bf"])
        self.copy("dve", self.ident_bf[:], self.cst[:, 1, :], ["cst"], ["ident_bf"])
        self.copy("dve", self.mask_bf[:], self.cst[:, 3, :], ["cst"], ["mask_bf"])
        self.p.add("dve", lambda e: e.memset(self.epsc[:], EPS), writes=["epsc"])
        self.p.add("dve", lambda e: e.memset(self.lru_h[:], 0.0), writes=[f"lru_h{j}_{c}" for j in range(2) for c in range(12)])
        self.p.add("dve", lambda e: e.memset(self.ctail[:], 0.0), writes=[f"ctail{l}_{c}" for l in range(DEPTH) for c in range(12)])
        lam = self.lruv[:, :, :, 7]
        self.act(self.nsp8[:], lam, AF.Exp, ["lruv"], ["nsp8"], scale=-1.0)
        self.act(self.nsp8[:], self.nsp8[:], AF.Ln, ["nsp8"], ["nsp8"], bias=1.0)
        self.p.add("act", lambda e: e.mul(self.nsp8[:], self.nsp8[:], -8.0), reads=["nsp8"], writes=["nsp8"])

        self.wgu_plan = []
        self.wd_plan = []
        for hf in range(2):
            for l in L:
                for fi in range(2):
                    for fc in range(FC):
                        self.wgu_plan.append((l, fi, fc))
                    for mc in range(8):
                        self.wd_plan.append((l, fi, mc))
        self.wgu_issued = 0
        self.wd_issued = 0
        self.wgu_used = 0
        self.wd_used = 0

        for hf in range(2):
            t0 = hf * TH
            self.dma("sp", self.hT[:], self.d_hin[:, :, t0:t0 + TH], [], self.hT_res(), "ldh")
            for l in L:
                self.ffn(l, 0, hf)
                self.mixer(l, hf)
                self.ffn(l, 1, hf)
            self.finish(hf)
        cnt = self.p.emit(nc, final_waits=["out0", "out1"])
        self.st.close()
        return nc, cnt

    def hT_res(self, kcs=range(8), tts=range(NT)):
        return [f"hT{kc}_{tt}" for kc in kcs for tt in tts]

    def wgu_prefetch(self, upto):
        while self.wgu_issued < min(upto, len(self.wgu_plan)):
            l, fi, fc = self.wgu_plan[self.wgu_issued]
            src = self.dr["wgu"][l, fi, fc]
            self.wgu.load(lambda t, src=src: (t[:], src))
            self.wgu_issued += 1

    def wd_prefetch(self, upto):
        while self.wd_issued < min(upto, len(self.wd_plan)):
            l, fi, mc = self.wd_plan[self.wd_issued]
            src = self.dr["wd"][l, fi, mc]
            self.wdr.load(lambda t, src=src: (t[:], src))
            self.wd_issued += 1

    def rmsnorm(self, src, src_res_fn, gcol0, dst, dst_res_fn, ntok, sq, rs):
        ntile = (ntok + 511) // 512
        for tt in range(ntile):
            w = min(512, ntok - tt * 512)
            sl = slice(tt * 512, tt * 512 + w)
            self.act(sq[:, :, 0:w], src[:, :, sl], AF.Square, src_res_fn(range(8), [tt]), ["A:sq"])
            pb = self.ps_next()
            for kc in range(8):
                self.mm(self.ps[pb][:, 0:w], self.ones_bf[:], sq[:, kc, 0:w], kc == 0, kc == 7,
                        ["A:sq", "ones_bf"], self.PS(pb))
            self.act(rs[:, 0:w], self.ps[pb][:, 0:w], AF.Sqrt, self.PS(pb) + ["epsc"], ["A:rs"],
                     bias=self.epsc[:, 0:1], scale=1.0 / D)
            self.recip(rs[:, 0:w], rs[:, 0:w], ["A:rs"], ["A:rs"])
            for kc in range(8):
                self.stt(dst[:, kc, sl], src[:, kc, sl], self.gall[:, gcol0 + kc:gcol0 + kc + 1], rs[:, 0:w],
                         ALU.mult, ALU.mult, src_res_fn([kc], [tt]) + ["A:rs", "gall"], dst_res_fn(kc, tt))

    def ffn(self, l, fi, hf):
        self.phase()
        xn = self.aalloc([128, 8, TH], BF16)
        hid = self.aalloc([128, FC, TH], BF16)
        sq = self.aalloc([128, 8, 512], BF16)
        rs = self.aalloc([128, 512], F32)
        sg = [self.aalloc([128, 512], F32) for _ in range(2)]
        gcol = l * 32 + (G_FFN1 if fi == 0 else G_FFN2)
        self.wgu_prefetch(self.wgu_used + 3)
        self.rmsnorm(self.hT, lambda kcs, tts: self.hT_res(kcs, tts), gcol, xn,
                     lambda kc, tt: [f"A:xn{kc}_{tt}"], TH, sq, rs)
        xn_all = [f"A:xn{kc}_{tt}" for kc in range(8) for tt in range(NT)]
        sgc = 0
        for fc in range(FC):
            s = self.wgu_used % 4
            self.wgu_used += 1
            self.wgu_prefetch(self.wgu_used + 3)
            if fc == FC - 3:
                self.wd_prefetch(self.wd_used + 2)
            wt = self.wgu.tiles[s]
            banks = [[self.ps_next() for _ in range(NT)] for _ in range(2)]
            for gu in range(2):
                for kc in range(8):
                    for tt in range(NT):
                        self.mm(self.ps[banks[gu][tt]][:], wt[:, gu, kc, :], xn[:, kc, tt * 512:(tt + 1) * 512],
                                kc == 0, kc == 7, [self.wgu.res[s], f"A:xn{kc}_{tt}"], self.PS(banks[gu][tt]))
            for tt in range(NT):
                sgt = sg[sgc % 2]
                sgr = f"A:sg{sgc % 2}"
                sgc += 1
                self.act(sgt[:], self.ps[banks[0][tt]][:], AF.Silu, self.PS(banks[0][tt]), [sgr])
                self.tt("dve", hid[:, fc, tt * 512:(tt + 1) * 512], self.ps[banks[1][tt]][:], sgt[:], ALU.mult,
                        self.PS(banks[1][tt]) + [sgr], [f"A:hid{fc}_{tt}"])
        for mc in range(8):
            s = self.wd_used % 3
            self.wd_used += 1
            self.wd_prefetch(self.wd_used + 2)
            wt = self.wdr.tiles[s]
            banks = [self.ps_next() for _ in range(NT)]
            for fc in range(FC):
                for tt in range(NT):
                    self.mm(self.ps[banks[tt]][:], wt[:, fc, :], hid[:, fc, tt * 512:(tt + 1) * 512],
                            fc == 0, fc == FC - 1, [self.wdr.res[s], f"A:hid{fc}_{tt}"], self.PS(banks[tt]))
            for tt in range(NT):
                sl = slice(tt * 512, (tt + 1) * 512)
                self.stt(self.hT[:, mc, sl], self.ps[banks[tt]][:], 0.5, self.hT[:, mc, sl], ALU.mult, ALU.add,
                         self.PS(banks[tt]) + [f"hT{mc}_{tt}"], [f"hT{mc}_{tt}"])

    def outproj(self, ysrc, yres, nk, wo_t, wo_res, tt):
        sl = slice(tt * 512, (tt + 1) * 512)
        for mc in range(8):
            pb = self.ps_next()
            for k in range(nk):
                self.mm(self.ps[pb][:], wo_t[:, k, mc * 128:(mc + 1) * 128], ysrc[:, k, :], k == 0, k == nk - 1,
                        [wo_res, yres], self.PS(pb))
            self.tt("dve", self.hT[:, mc, sl], self.ps[pb][:], self.hT[:, mc, sl], ALU.add,
                    self.PS(pb) + [f"hT{mc}_{tt}"], [f"hT{mc}_{tt}"])

    def mixer(self, l, hf):
        self.phase()
        j = l // 2
        is_lru = (l % 2 == 0)
        self.wor = Ring(self, "wo", 2, [128, 4, 1024], BF16, arena=True)
        if not is_lru:
            self.ml_alloc()
        xn = self.aalloc([128, 8, TH], BF16)
        sq = self.aalloc([128, 8, 512], BF16)
        rs = self.aalloc([128, 512], F32)
        self.win = Ring(self, "win", 3, [128, 8, 512], BF16, arena=True)
        self.rmsnorm(self.hT, lambda kcs, tts: self.hT_res(kcs, tts), l * 32 + G_MIX, xn,
                     lambda kc, tt: [f"A:xn{kc}_{tt}"], TH, sq, rs)
        self.xn = xn
        off_x = self.ar_off
        self.xattn(l, sq, rs)
        self.ar_off = off_x
        scr = self.bar_scr
        self.p.barrier(lambda e: e.memset(scr[:, 0:8], 0.0))
        if is_lru:
            self.lru(l, j, hf)
        else:
            self.mlstm(l, j, hf)

    def xattn(self, l, sq, rs):
        memn = self.aalloc([128, 8, 256], BF16)
        kT = self.aalloc([128, 4, 256], BF16)
        vm = self.aalloc([128, 2, 512], BF16)
        qh = [self.aalloc([128, 512], BF16) for _ in range(2)]
        ex = [self.aalloc([128, 2, 512], BF16) for _ in range(2)]
        rden = self.aalloc([128, 512], F32)
        xa = [self.aalloc([128, 4, 512], BF16)]
        sk = self.win.load(lambda t: (t[:], self.dr["wkv_k"][l]))
        wkv, wkvr = self.win.tiles[sk], self.win.res[sk]
        sq_ = self.win.load(lambda t: (t[:], self.dr["win_q"][l]))
        winq, winqr = self.win.tiles[sq_], self.win.res[sq_]
        ws = self.wor.load(lambda t: (t[:], self.dr["wo"][l, :, 12:16, :]))
        self.rmsnorm(self.memT, lambda kcs, tts: ["memT"], l * 32 + G_MEM, memn,
                     lambda kc, tt: ["A:memn"], 256, sq, rs)
        for h in range(4):
            pb = self.ps_next()
            for kc in range(8):
                self.mm(self.ps[pb][:, 0:256], wkv[:, kc, h * 128:(h + 1) * 128], memn[:, kc, :], kc == 0, kc == 7,
                        [wkvr, "A:memn"], self.PS(pb))
            self.copy("act", kT[:, h, :], self.ps[pb][:, 0:256], self.PS(pb), ["A:kT"])
        sv = self.win.load(lambda t: (t[:], self.dr["wkv_v"][l]))
        wkv, wkvr = self.win.tiles[sv], self.win.res[sv]
        for mcm in range(2):
            pb = self.ps_next()
            for kc in range(8):
                self.mm(self.ps[pb][:], memn[:, kc, mcm * 128:(mcm + 1) * 128], wkv[:, kc, :], kc == 0, kc == 7,
                        [wkvr, "A:memn"], self.PS(pb))
            self.copy("act", vm[:, mcm, :], self.ps[pb][:], self.PS(pb), ["A:vm"])
        c = 0
        for tt in range(NT):
            sl = slice(tt * 512, (tt + 1) * 512)
            xat = xa[0]
            xar = "A:xa0"
            for h in range(4):
                q_t, q_r = qh[c % 2], f"A:qh{c % 2}"
                e_t, e_r = ex[c % 2], f"A:ex{c % 2}"
                c += 1
                pb = self.ps_next()
                for kc in range(8):
                    self.mm(self.ps[pb][:], winq[:, kc, h * 128:(h + 1) * 128], self.xn[:, kc, sl], kc == 0, kc == 7,
                            [winqr, f"A:xn{kc}_{tt}"], self.PS(pb))
                self.copy("act", q_t[:], self.ps[pb][:], self.PS(pb), [q_r])
                for mcm in range(2):
                    pb = self.ps_next()
                    self.mm(self.ps[pb][:], kT[:, h, mcm * 128:(mcm + 1) * 128], q_t[:], True, True,
                            ["A:kT", q_r], self.PS(pb))
                    self.act(e_t[:, mcm, :], self.ps[pb][:], AF.Exp, self.PS(pb), [e_r + f"_{mcm}"], scale=XA_SCALE)
                po = self.ps_next()
                pd = self.ps_next()
                for mcm in range(2):
                    self.mm(self.ps[po][:], vm[:, mcm, h * 128:(h + 1) * 128], e_t[:, mcm, :], mcm == 0, mcm == 1,
                            ["A:vm", e_r + f"_{mcm}"], self.PS(po))
                for mcm in range(2):
                    self.mm(self.ps[pd][:], self.ones_bf[:], e_t[:, mcm, :], mcm == 0, mcm == 1,
                            ["ones_bf", e_r + f"_{mcm}"], self.PS(pd))
                self.recip(rden[:], self.ps[pd][:], self.PS(pd), ["A:rden"])
                self.tt("dve", xat[:, h, :], self.ps[po][:], rden[:], ALU.mult, self.PS(po) + ["A:rden"], [xar])
            self.outproj(xat, xar, 4, self.wor.tiles[ws], self.wor.res[ws], tt)

    def conv(self, l, cg, vec, mainp, mres, xc, xres):
        w = lambda t: vec[:, cg, t:t + 1]
        self.copy("pool", mainp[:, 0:3], self.ctail[:, l, cg, :], [f"ctail{l}_{cg}"], [mres + "h"])
        self.ts("dve", xc, mainp[:, 0:512], w(0), w(4), ALU.mult, ALU.add, [mres, mres + "h"], [xres])
        for t in range(1, 4):
            self.stt(xc, mainp[:, t:t + 512], w(t), xc, ALU.mult, ALU.add, [mres, mres + "h", xres], [xres])
        self.copy("pool", self.ctail[:, l, cg, :], mainp[:, 512:515], [mres], [f"ctail{l}_{cg}"])

    def main_proj(self, wm, wm_res, c, tt, mainp, mres):
        sl = slice(tt * 512, (tt + 1) * 512)
        pb = self.ps_next()
        for kc in range(8):
            self.mm(self.ps[pb][:], wm[:, kc, c * 128:(c + 1) * 128], self.xn[:, kc, sl], kc == 0, kc == 7,
                    [wm_res, f"A:xn{kc}_{tt}"], self.PS(pb))
        self.copy("act", mainp[:, 3:515], self.ps[pb][:], self.PS(pb), [mres])

    def lru(self, l, j, hf):
        vec = self.lruv[:, j]
        lrw = Ring(self, "lrw", 2, [128, 2, 3, 384], BF16, arena=True)
        mainp = [self.aalloc([128, 515], F32) for _ in range(3)]
        xc = [self.aalloc([128, 512], F32) for _ in range(3)]
        xcb = [self.aalloc([128, 512], BF16) for _ in range(3)]
        A = [self.aalloc([128, 512], F32) for _ in range(2)]
        T1 = [self.aalloc([128, 512], F32) for _ in range(2)]
        T2 = [self.aalloc([128, 512], F32) for _ in range(2)]
        HH = [self.aalloc([128, 512], F32) for _ in range(2)]
        GG = [self.aalloc([128, 512], F32) for _ in range(2)]
        yb = [self.aalloc([128, 3, 512], BF16) for _ in range(2)]
        it = 0
        ic = 0
        for g in range(4):
            sm = self.win.load(lambda t: (t[:, :, 0:384], self.dr["win_m"][l, g]))
            sg_ = self.win.load(lambda t: (t[:, :, 0:384], self.dr["win_g"][l, g]))
            sw = lrw.load(lambda t: (t[:], self.dr["lrw"][j, g]))
            so = self.wor.load(lambda t: (t[:, 0:3, :], self.dr["wo"][l, :, 3 * g:3 * g + 3, :]))
            wm, wg = self.win.tiles[sm], self.win.tiles[sg_]
            wmr, wgr = self.win.res[sm], self.win.res[sg_]
            wl, wlr = lrw.tiles[sw], lrw.res[sw]
            for tt in range(NT):
                sl = slice(tt * 512, (tt + 1) * 512)
                y_t, y_r = yb[it % 2], f"A:y{it % 2}"
                it += 1
                for c in range(3):
                    cg = 3 * g + c
                    self.main_proj(wm, wmr, c, tt, mainp[c], f"A:mp{c}")
                    self.conv(l, cg, vec, mainp[c], f"A:mp{c}", xc[c][:], f"A:xc{c}")
                    self.copy("pool", xcb[c][:], xc[c][:], [f"A:xc{c}"], [f"A:xcb{c}"])
                for c in range(3):
                    cg = 3 * g + c
                    k = ic % 2
                    ic += 1
                    a_t, t1, t2, hh, gg = A[k], T1[k], T2[k], HH[k], GG[k]
                    ar, t1r, t2r, hr, gr = f"A:a{k}", f"A:t1{k}", f"A:t2{k}", f"A:hh{k}", f"A:gg{k}"
                    pb = self.ps_next()
                    for kk in range(3):
                        self.mm(self.ps[pb][:], wl[:, 0, kk, c * 128:(c + 1) * 128], xcb[kk][:], kk == 0, kk == 2,
                                [wlr, f"A:xcb{kk}"], self.PS(pb))
                    self.act(a_t[:], self.ps[pb][:], AF.Sigmoid, self.PS(pb) + ["lruv"], [ar], bias=vec[:, cg, 5:6])
                    self.act(a_t[:], a_t[:], AF.Exp, [ar, "nsp8"], [ar], scale=self.nsp8[:, j, cg:cg + 1])
                    pb = self.ps_next()
                    for kk in range(3):
                        self.mm(self.ps[pb][:], wl[:, 1, kk, c * 128:(c + 1) * 128], xcb[kk][:], kk == 0, kk == 2,
                                [wlr, f"A:xcb{kk}"], self.PS(pb))
                    self.act(t2[:], self.ps[pb][:], AF.Sigmoid, self.PS(pb) + ["lruv"], [t2r], bias=vec[:, cg, 6:7])
                    self.act(t1[:], a_t[:], AF.Square, [ar], [t1r])
                    self.act(t1[:], t1[:], AF.Sqrt, [t1r], [t1r], bias=1.0, scale=-1.0)
                    self.tt("dve", t2[:], t2[:], xc[c][:], ALU.mult, [t2r, f"A:xc{c}"], [t2r])
                    self.tt("dve", t2[:], t2[:], t1[:], ALU.mult, [t2r, t1r], [t2r])
                    hst = self.lru_h[:, j, cg:cg + 1]
                    self.p.add("dve", lambda e, hh=hh, a_t=a_t, t2=t2, hst=hst: e.tensor_tensor_scan(
                        out=hh[:], data0=a_t[:], data1=t2[:], initial=hst, op0=ALU.mult, op1=ALU.add),
                        reads=[ar, t2r, f"lru_h{j}_{cg}"], writes=[hr])
                    self.copy("pool", hst, hh[:, 511:512], [hr], [f"lru_h{j}_{cg}"])
                    pb = self.ps_next()
                    for kc in range(8):
                        self.mm(self.ps[pb][:], wg[:, kc, c * 128:(c + 1) * 128], self.xn[:, kc, sl], kc == 0, kc == 7,
                                [wgr, f"A:xn{kc}_{tt}"], self.PS(pb))
                    self.act(gg[:], self.ps[pb][:], AF.Gelu_apprx_tanh, self.PS(pb), [gr])
                    self.tt("dve", y_t[:, c, :], hh[:], gg[:], ALU.mult, [hr, gr], [y_r])
                self.outproj(y_t, y_r, 3, self.wor.tiles[so], self.wor.res[so], tt)

    def ml_alloc(self):
        bd = self.aalloc([128, 3, 12, 128], BF16)
        wgs = self.aalloc([128, 36, 8], BF16)
        wc = self.aalloc([128, 24, 8], BF16)
        gT = self.aalloc([8, TH], F32)
        gtm = self.aalloc([128, 8, 8], F32)
        logf = self.aalloc([128, 8, 4], F32)
        bias_s = self.aalloc([128, 8, 4], F32)
        wS = self.aalloc([128, 8, 4], F32)
        eB = self.aalloc([128, 8, 4], F32)
        self.ml_t = (bd, wgs, wc, gT, gtm, logf, bias_s, wS, eB)
        self.ml_off = self.ar_off

    def mlstm(self, l, j, hf):
        vec = self.mlv[:, j]
        bd, wgs, wc, gT, gtm, logf, bias_s, wS, eB = self.ml_t
        off_pass = self.ml_off
        bdT = self.aalloc([128, 3, 12, 128], BF16)
        self.dma("pool", bd[:], self.dr["mlbd"][j], [], ["A:bd"], "bd")
        self.dma("pool", bdT[:], self.dr["mlbdT"][j], [], ["A:bdT"], "bdT")
        self.dma("pool", wgs[:], self.dr["mlwg"][j], [], ["A:wgs"], "wgs")
        pb = self.ps_next()
        for c in range(12):
            o = self.ps[pb][:, c * 8:(c + 1) * 8]
            self.mm(o, bdT[:, 0, c, :], wgs[:, c, :], True, False, ["A:bdT", "A:wgs"], self.PS(pb))
            self.mm(o, bdT[:, 1, c, :], wgs[:, 12 + c, :], False, True, ["A:bdT", "A:wgs"], self.PS(pb))
        for c in range(12):
            o = self.ps[pb][:, (12 + c) * 8:(13 + c) * 8]
            self.mm(o, bdT[:, 2, c, :], wgs[:, 24 + c, :], True, True, ["A:bdT", "A:wgs"], self.PS(pb))
        self.copy("act", wc[:].rearrange("p a b -> p (a b)"), self.ps[pb][:, 0:192], self.PS(pb), ["A:wc"])
        mainp = [self.aalloc([128, 515], F32) for _ in range(3)]
        xc = [self.aalloc([128, 512], F32) for _ in range(3)]
        xcb = [self.aalloc([128, 512], BF16) for _ in range(3)]
        mb = [self.aalloc([128, 512], BF16) for _ in range(3)]
        sgt = [self.aalloc([128, 512], F32) for _ in range(2)]
        gbank = [self.ps_next() for _ in range(NT)]
        for b_ in gbank:
            self.ps_reserved.add(b_)
        ib = 0
        isg = 0
        for g in range(4):
            sm = self.win.load(lambda t: (t[:, :, 0:384], self.dr["win_m"][l, g]))
            sg_ = self.win.load(lambda t: (t[:, :, 0:384], self.dr["win_g"][l, g]))
            wm, wg = self.win.tiles[sm], self.win.tiles[sg_]
            wmr, wgr = self.win.res[sm], self.win.res[sg_]
            for tt in range(NT):
                sl = slice(tt * 512, (tt + 1) * 512)
                for c in range(3):
                    cg = 3 * g + c
                    k = ib % 3
                    ib += 1
                    xcr, xcbr, mbr = f"A:xc{k}", f"A:xcb{k}", f"A:mb{k}"
                    self.main_proj(wm, wmr, c, tt, mainp[c], f"A:mp{c}")
                    self.conv(l, cg, vec, mainp[c], f"A:mp{c}", xc[k][:], xcr)
                    self.act(xc[k][:], xc[k][:], AF.Silu, [xcr], [xcr])
                    self.copy("pool", xcb[k][:], xc[k][:], [xcr], [xcbr])
                    self.copy("pool", mb[k][:], mainp[c][:, 3:515], [f"A:mp{c}"], [mbr])
                    first = (g == 0 and c == 0)
                    lastm = (g == 3 and c == 2)
                    self.mm(self.ps[gbank[tt]][0:8, :], wc[:, cg, :], xcb[k][:], first, False,
                            ["A:wc", xcbr], self.PS(gbank[tt]))
                    self.mm(self.ps[gbank[tt]][0:8, :], wc[:, 12 + cg, :], mb[k][:], False, lastm,
                            ["A:wc", mbr], self.PS(gbank[tt]))
                    self.dma("sp", self.scr_xc[cg, tt], xc[k][:], [xcr], [f"D:xc{cg}_{tt}"], f"sxc{k}")
                    self.dma("sp", self.scr_xcb[cg, tt], xcb[k][:], [xcbr], [f"D:xcb{cg}_{tt}"], f"sxb{k}")
                    self.dma("sp", self.scr_mb[cg, tt], mb[k][:], [mbr], [f"D:mb{cg}_{tt}"], f"smb{k}")
                    ks = isg % 2
                    isg += 1
                    pb = self.ps_next()
                    for kc in range(8):
                        self.mm(self.ps[pb][:], wg[:, kc, c * 128:(c + 1) * 128], self.xn[:, kc, sl], kc == 0, kc == 7,
                                [wgr, f"A:xn{kc}_{tt}"], self.PS(pb))
                    self.act(sgt[ks][:], self.ps[pb][:], AF.Silu, self.PS(pb), [f"A:sgt{ks}"])
                    self.dma("sp", self.scr_sg[cg, tt], sgt[ks][:], [f"A:sgt{ks}"], [f"D:sg{cg}_{tt}"], f"ssg{ks}")
        for tt in range(NT):
            self.act(gT[:, tt * 512:(tt + 1) * 512], self.ps[gbank[tt]][0:8, :], AF.Identity,
                     self.PS(gbank[tt]) + ["mlbg"], ["A:gT"], bias=self.mlbg[:, j:j + 1])
            self.ps_reserved.discard(gbank[tt])
        pb = self.ps_next()
        for jc in range(8):
            self.p.add("pe", lambda e, jc=jc, pb=pb: e.transpose(
                out=self.ps[pb][:, jc * 8:(jc + 1) * 8], in_=gT[0:8, jc * 128:(jc + 1) * 128],
                identity=self.ident_f[0:8, 0:8]), reads=["A:gT", "cst"], writes=self.PS(pb))
        self.copy("act", gtm[:].rearrange("p a b -> p (a b)"), self.ps[pb][:, 0:64], self.PS(pb), ["A:gtm"])
        self.act(logf[:], gtm[:, :, 4:8], AF.Exp, ["A:gtm"], ["A:logf"], scale=-1.0)
        self.act(logf[:], logf[:], AF.Ln, ["A:logf"], ["A:logf"], bias=1.0)
        self.p.add("act", lambda e: e.mul(logf[:], logf[:], -1.0), reads=["A:logf"], writes=["A:logf"])
        lf2 = logf[:].rearrange("p a b -> p (a b)")
        pb = self.ps_next()
        self.mm(self.ps[pb][:, 0:32], self.tri_f, lf2, True, True, ["cst", "A:logf"], self.PS(pb))
        self.mm(self.ps[pb][:, 32:64], self.ones_f, lf2, True, True, ["cst", "A:logf"], self.PS(pb))
        bs2 = bias_s[:].rearrange("p a b -> p (a b)")
        self.stt(bias_s[:], gtm[:, :, 0:4], LN_SCALE, self.ps[pb][:, 0:32].rearrange("p (a b) -> p a b", b=4),
                 ALU.add, ALU.subtract, ["A:gtm"] + self.PS(pb), ["A:bias_s"])
        self.tt("dve", wS[:].rearrange("p a b -> p (a b)"), self.ps[pb][:, 32:64], bs2, ALU.add,
                ["A:bias_s"] + self.PS(pb), ["A:wS"])
        self.act(wS[:], wS[:], AF.Exp, ["A:wS"], ["A:wS"])
        self.act(eB[:].rearrange("p a b -> p (a b)"), self.ps[pb][:, 32:64], AF.Exp, self.PS(pb), ["A:eB"])
        self.ar_off = off_pass
        scr = self.bar_scr
        self.p.barrier(lambda e: e.memset(scr[:, 0:8], 0.0))
        xcl = [self.aalloc([128, 3, 512], F32) for _ in range(2)]
        sgl = [self.aalloc([128, 3, 512], F32) for _ in range(2)]
        xbl = [self.aalloc([128, 3, 512], BF16) for _ in range(2)]
        mbl = [self.aalloc([128, 3, 512], BF16) for _ in range(2)]
        qT = self.aalloc([128, 3, 512], BF16)
        kTt = self.aalloc([128, 3, 512], BF16)
        vext = self.aalloc([128, 4, 512], BF16)
        hh = self.aalloc([128, 3, 512], F32)
        hb = self.aalloc([128, 3, 512], BF16)
        hsq = self.aalloc([128, 3, 512], BF16)
        mean = self.aalloc([128, 512], F32)
        msq = self.aalloc([128, 512], F32)
        rstd = self.aalloc([128, 512], F32)
        tmp = [self.aalloc([128, 512], F32) for _ in range(2)]
        yb = [self.aalloc([128, 3, 512], BF16) for _ in range(2)]
        triL = [self.aalloc([128, 128], F32) for _ in range(2)]
        ebr = [self.aalloc([128, 128], F32) for _ in range(2)]
        Dm = [self.aalloc([128, 128], F32) for _ in range(2)]
        ST = [self.aalloc([128, 128], BF16) for _ in range(2)]
        qs = [self.aalloc([128, 3, 128], BF16) for _ in range(2)]
        kw = [self.aalloc([128, 384], BF16) for _ in range(2)]
        dd = [self.aalloc([128, 128], F32) for _ in range(2)]
        ctf = self.aalloc([128, 3, 512], F32)
        ctb = self.aalloc([128, 3, 512], BF16)
        self.p.add("pool", lambda e: e.memset(vext[:, :, 384:512], 1.0), writes=["A:vext_one"])
        it = 0
        istep = 0
        for g in range(4):
            so = self.wor.load(lambda t: (t[:, 0:3, :], self.dr["wo"][l, :, 3 * g:3 * g + 3, :]))
            if hf == 1:
                self.dma("sp", ctf[:], self.scr_ct[j, g], [f"D:ct{j}_{g}"], ["A:ctf"], "lct")
                self.copy("pool", ctb[:], ctf[:], ["A:ctf"], ["A:ctb"])
            for tt in range(NT):
                k2 = it % 2
                it += 1
                y_t, y_r = yb[k2], f"A:y{k2}"
                cgs = [3 * g + c for c in range(3)]
                c0 = cgs[0]
                for buf, dsc, nm in ((xcl, self.scr_xc, "xc"), (sgl, self.scr_sg, "sg"), (xbl, self.scr_xcb, "xcb"), (mbl, self.scr_mb, "mb")):
                    lnm = {"xc": "xcl", "sg": "sgl", "xcb": "xbl", "mb": "mbl"}[nm]
                    self.dma("sp", buf[k2][:], dsc[c0:c0 + 3, tt].rearrange("c p t -> p c t"),
                             [f"D:{nm}{cg}_{tt}" for cg in cgs], [f"A:{lnm}{k2}_{c}" for c in range(3)], f"l{nm}{k2}")
                for c in range(3):
                    cg = cgs[c]
                    pb = self.ps_next()
                    self.mm(self.ps[pb][:], bd[:, 0, cg, :], xbl[k2][:, c, :], True, True, ["A:bd", f"A:xbl{k2}_{c}"], self.PS(pb))
                    self.copy("act", qT[:, c, :], self.ps[pb][:], self.PS(pb), [f"A:qT{c}"])
                    pb = self.ps_next()
                    self.mm(self.ps[pb][:], bd[:, 1, cg, :], xbl[k2][:, c, :], True, True, ["A:bd", f"A:xbl{k2}_{c}"], self.PS(pb))
                    self.copy("act", kTt[:, c, :], self.ps[pb][:], self.PS(pb), [f"A:kT{c}"])
                for jc in range(4):
                    pb = self.ps_next()
                    for c in range(3):
                        self.mm(self.ps[pb][:, c * 128:(c + 1) * 128], mbl[k2][:, c, jc * 128:(jc + 1) * 128],
                                bd[:, 2, cgs[c], :], True, True, ["A:bd", f"A:mbl{k2}_{c}"], self.PS(pb))
                    self.copy("act", vext[:, jc, 0:384], self.ps[pb][:, 0:384], self.PS(pb), [f"A:vext{jc}"])
                for jc in range(4):
                    jcg = tt * 4 + jc
                    first_chunk = (hf == 0 and jcg == 0)
                    last_chunk = (hf == 1 and jcg == 7)
                    ks = istep % 2
                    istep += 1
                    cs = slice(jc * 128, (jc + 1) * 128)
                    colp = lambda t_: t_[:, jcg, g:g + 1]
                    pk = self.ps_next()
                    for c in range(3):
                        self.mm(self.ps[pk][:, c * 128:(c + 1) * 128], xbl[k2][:, c, cs], bd[:, 1, cgs[c], :], True, True,
                                ["A:bd", f"A:xbl{k2}_{c}"], self.PS(pk))
                    self.act(kw[ks][:], self.ps[pk][:, 0:384], AF.Identity, self.PS(pk) + ["A:wS"], [f"A:kw{ks}"],
                             scale=colp(wS))
                    self.ts("pool", triL[ks][:], self.tri_f, colp(logf), None, ALU.mult, None, ["cst", "A:logf"], [f"A:triL{ks}"])
                    pa = self.ps_next()
                    self.mm(self.ps[pa][:, 0:128], self.ones_f, triL[ks][:], True, True, ["cst", f"A:triL{ks}"], self.PS(pa))
                    self.mm(self.ps[pa][:, 128:256], self.ones_f, triL[ks][:], True, False, ["cst", f"A:triL{ks}"], self.PS(pa))
                    self.mm(self.ps[pa][:, 128:256], self.ident_bf[:], self.mask_bf[:], False, True,
                            ["ident_bf", "mask_bf"], self.PS(pa))
                    for c in range(3):
                        self.mm(self.ps[pa][:, 256:384], kTt[:, c, cs], qT[:, c, cs], c == 0, c == 2,
                                [f"A:kT{c}", f"A:qT{c}"], self.PS(pa))
                    self.act(ebr[ks][:], self.ps[pa][:, 0:128], AF.Exp, self.PS(pa), [f"A:ebr{ks}"])
                    self.act(Dm[ks][:], self.ps[pa][:, 128:256], AF.Exp, self.PS(pa) + ["A:bias_s"], [f"A:Dm{ks}"],
                             bias=colp(bias_s))
                    self.tt("dve", ST[ks][:], self.ps[pa][:, 256:384], Dm[ks][:], ALU.mult, self.PS(pa) + [f"A:Dm{ks}"], [f"A:ST{ks}"])
                    if not first_chunk:
                        for c in range(3):
                            self.tt("dve", qs[ks][:, c, :], qT[:, c, cs], ebr[ks][:], ALU.mult,
                                    [f"A:qT{c}", f"A:ebr{ks}"], [f"A:qs{ks}"])
                    pn = self.ps_next()
                    for cc in range(4):
                        o = self.ps[pn][:, cc * 128:(cc + 1) * 128]
                        vres = f"A:vext{jc}" if cc < 3 else "A:vext_one"
                        self.mm(o, vext[:, jc, cc * 128:(cc + 1) * 128], ST[ks][:], True, first_chunk,
                                [vres, f"A:ST{ks}"], self.PS(pn))
                        if not first_chunk:
                            for dk in range(3):
                                self.mm(o, ctb[:, dk, cc * 128:(cc + 1) * 128], qs[ks][:, dk, :], False, dk == 2,
                                        ["A:ctb", f"A:qs{ks}"], self.PS(pn))
                    self.act(dd[ks][:], self.ps[pn][:, 384:512], AF.Abs, self.PS(pn), [f"A:dd{ks}"])
                    self.ts("dve", dd[ks][:], dd[ks][:], 1.0, None, ALU.max, None, [f"A:dd{ks}"], [f"A:dd{ks}"])
                    self.recip(dd[ks][:], dd[ks][:], [f"A:dd{ks}"], [f"A:dd{ks}"])
                    for c in range(3):
                        self.tt("dve", hh[:, c, cs], self.ps[pn][:, c * 128:(c + 1) * 128], dd[ks][:], ALU.mult,
                                self.PS(pn) + [f"A:dd{ks}"], [f"A:hh{c}"])
                    if not last_chunk:
                        for dk in range(3):
                            pc = self.ps_next()
                            self.mm(self.ps[pc][:], kw[ks][:, dk * 128:(dk + 1) * 128], vext[:, jc, :], True, True,
                                    [f"A:kw{ks}", f"A:vext{jc}", "A:vext_one"], self.PS(pc))
                            if first_chunk:
                                self.copy("dve", ctf[:, dk, :], self.ps[pc][:], self.PS(pc), ["A:ctf"])
                            else:
                                self.stt(ctf[:, dk, :], ctf[:, dk, :], colp(eB), self.ps[pc][:], ALU.mult, ALU.add,
                                         ["A:ctf", "A:eB"] + self.PS(pc), ["A:ctf"])
                        self.copy("pool", ctb[:], ctf[:], ["A:ctf"], ["A:ctb"])
                hres = [f"A:hh{c}" for c in range(3)]
                self.copy("act", hb[:], hh[:], hres, ["A:hb"])
                self.act(hsq[:], hh[:], AF.Square, hres, ["A:hsq"])
                pm = self.ps_next()
                for c in range(3):
                    self.mm(self.ps[pm][:], self.ones_bf[:], hb[:, c, :], c == 0, c == 2, ["ones_bf", "A:hb"], self.PS(pm))
                p2 = self.ps_next()
                for c in range(3):
                    self.mm(self.ps[p2][:], self.ones_bf[:], hsq[:, c, :], c == 0, c == 2, ["ones_bf", "A:hsq"], self.PS(p2))
                self.p.add("act", lambda e, pm=pm: e.mul(mean[:], self.ps[pm][:], 1.0 / 384.0), reads=self.PS(pm), writes=["A:mean"])
                self.act(msq[:], mean[:], AF.Square, ["A:mean"], ["A:msq"])
                self.stt(rstd[:], self.ps[p2][:], 1.0 / 384.0, msq[:], ALU.mult, ALU.subtract, self.PS(p2) + ["A:msq"], ["A:rstd"])
                self.act(rstd[:], rstd[:], AF.Sqrt, ["A:rstd", "epsc"], ["A:rstd"], bias=self.epsc[:, 0:1])
                self.recip(rstd[:], rstd[:], ["A:rstd"], ["A:rstd"])
                for c in range(3):
                    cg = cgs[c]
                    kt = c % 2
                    t_ = tmp[kt]
                    tr = f"A:tmp{kt}"
                    self.tt("dve", t_[:], hh[:, c, :], mean[:], ALU.subtract, [f"A:hh{c}", "A:mean"], [tr])
                    self.stt(t_[:], t_[:], vec[:, cg, 5:6], rstd[:], ALU.mult, ALU.mult, [tr, "A:rstd", "mlv"], [tr])
                    self.stt(t_[:], xcl[k2][:, c, :], vec[:, cg, 6:7], t_[:], ALU.mult, ALU.add, [tr, f"A:xcl{k2}_{c}", "mlv"], [tr])
                    self.tt("dve", y_t[:, c, :], t_[:], sgl[k2][:, c, :], ALU.mult, [tr, f"A:sgl{k2}_{c}"], [y_r])
                self.outproj(y_t, y_r, 3, self.wor.tiles[so], self.wor.res[so], tt)
            if hf == 0:
                self.dma("sp", self.scr_ct[j, g], ctf[:], ["A:ctf"], [f"D:ct{j}_{g}"], "sct")

    def finish(self, hf):
        self.phase()
        t0 = hf * TH
        if self.last:
            sq = self.aalloc([128, 8, 512], BF16)
            rs = self.aalloc([128, 512], F32)
            on = self.aalloc([128, 8, TH], F32)
            for tt in range(NT):
                sl = slice(tt * 512, (tt + 1) * 512)
                self.act(sq[:], self.hT[:, :, sl], AF.Square, self.hT_res(range(8), [tt]), ["A:sq"])
                pb = self.ps_next()
                for kc in range(8):
                    self.mm(self.ps[pb][:], self.ones_bf[:], sq[:, kc, :], kc == 0, kc == 7, ["A:sq", "ones_bf"], self.PS(pb))
                self.act(rs[:], self.ps[pb][:], AF.Sqrt, self.PS(pb) + ["epsc"], ["A:rs"], bias=self.epsc[:, 0:1], scale=1.0 / D)
                self.recip(rs[:], rs[:], ["A:rs"], ["A:rs"])
                for kc in range(8):
                    self.stt(on[:, kc, sl], self.hT[:, kc, sl], self.gall[:, 128 + kc:129 + kc], rs[:], ALU.mult, ALU.mult,
                             [f"hT{kc}_{tt}", "A:rs", "gall"], [f"A:on{tt}"])
            self.dma("sp", self.d_out[:, :, t0:t0 + TH], on[:], [f"A:on{tt}" for tt in range(NT)], [], f"out{hf}")
        else:
            self.dma("sp", self.d_out[:, :, t0:t0 + TH], self.hT[:], self.hT_res(), [], f"out{hf}")


_PROG_CACHE = {}


def _get_prog(layers, first, last):
    key = (tuple(layers), first, last)
    if key not in _PROG_CACHE:
        b = Builder(list(layers), first, last)
        nc, cnt = b.build()
        _PROG_CACHE[key] = nc
    return _PROG_CACHE[key]


LAUNCH_PLAN = [[0, 1, 2, 3]]


def run_layers(shared, h_list, mem_list, layers, last, cores=NCORES, trace=False):
    nc = _get_prog(layers, layers[0] == 0, last)
    in_maps = []
    for c in range(cores):
        m = dict(shared)
        m["hin"] = h_list[c]
        m["memT"] = mem_list[c]
        in_maps.append(m)
    res = run_bass_kernel_spmd(nc, in_maps, core_ids=list(range(cores)), trace=trace)
    return [r["hout"] for r in res.results], res


def kernel(**inputs):
    x = np.asarray(inputs["x"], np.float32)
    mem = np.asarray(inputs["mem"], np.float32)
    shared = prep_shared(inputs)
    hs, ms = [], []
    for b in range(NCORES):
        xin, memT = prep_core(x[b], mem[b])
        hs.append(xin)
        ms.append(memT)
    for i, layers in enumerate(LAUNCH_PLAN):
        hs, _ = run_layers(shared, hs, ms, layers, last=(i == len(LAUNCH_PLAN) - 1))
    out = np.empty((NCORES, S, D), np.float32)
    for b in range(NCORES):
        out[b] = hs[b].transpose(2, 1, 0).reshape(S, D)
    return out
```

```python
import math
from contextlib import ExitStack

import numpy as np
import concourse.bass as bass
import concourse.mybir as mybir
from concourse.bass_utils import run_bass_kernel_spmd

F32 = mybir.dt.float32
BF16 = mybir.dt.bfloat16
AF = mybir.ActivationFunctionType
ALU = mybir.AluOpType

D = 1024
S = 2048
TH = 1024
NT = 2
FC = 22
DEPTH = 4
EPS = 1e-6
LN_SCALE = math.log(384.0 ** -0.5)
XA_SCALE = 128.0 ** -0.5
NCORES = 8

G_FFN1, G_MIX, G_FFN2, G_MEM = 0, 8, 16, 24


class Op:
    __slots__ = ("eng", "fn", "reads", "writes", "dma", "waits", "inc", "seq", "cum", "barrier", "dur", "issue")

    def __init__(self, eng, fn, reads, writes, dma, barrier=False, dur=500.0, issue=None):
        self.eng = eng
        self.fn = fn
        self.reads = tuple(reads)
        self.writes = tuple(writes)
        self.dma = dma
        self.waits = []
        self.inc = False
        self.seq = 0
        self.cum = 0
        self.barrier = barrier
        self.dur = dur
        self.issue = dur if issue is None else issue


SCHEDULE = True


class Prog:
    ENGS = ("pe", "act", "dve", "pool", "sp")

    def __init__(self):
        self.ops = []

    def add(self, eng, fn, reads=(), writes=(), dma=None, dur=500.0, issue=None):
        self.ops.append(Op(eng, fn, reads, writes, dma, dur=dur, issue=issue))

    def barrier(self, fn):
        self.ops.append(Op("pool", fn, (), (), None, barrier=True, dur=300.0))

    def _deps(self):
        ops = self.ops
        last_w = {}
        readers = {}
        bar_idx = None
        preds = []
        for i, op in enumerate(ops):
            deps = set()
            if op.barrier:
                for r in list(last_w.keys()):
                    if r.startswith("A:"):
                        deps.add(last_w.pop(r))
                for r in list(readers.keys()):
                    if r.startswith("A:"):
                        deps.update(readers.pop(r))
                if bar_idx is not None:
                    deps.add(bar_idx)
                bar_idx = i
            else:
                arena = False
                for r in op.reads:
                    if r.startswith("A:"):
                        arena = True
                    if r in last_w:
                        deps.add(last_w[r])
                for w in op.writes:
                    if w.startswith("A:"):
                        arena = True
                    if w in last_w:
                        deps.add(last_w[w])
                    deps.update(readers.get(w, ()))
                if arena and bar_idx is not None:
                    deps.add(bar_idx)
            deps.discard(i)
            preds.append(deps)
            for r in op.reads:
                readers.setdefault(r, []).append(i)
            for w in op.writes:
                last_w[w] = i
                readers[w] = []
        return preds

    def _schedule(self, preds):
        import heapq
        ops = self.ops
        n = len(ops)
        succs = [[] for _ in range(n)]
        npred = [0] * n
        for i, ps_ in enumerate(preds):
            npred[i] = len(ps_)
            for j in ps_:
                succs[j].append(i)
        finish = [0.0] * n
        ready_t = [0.0] * n
        tE = {e: 0.0 for e in self.ENGS}
        pend = {e: [] for e in self.ENGS}
        rdy = {e: [] for e in self.ENGS}
        for i in range(n):
            if npred[i] == 0:
                heapq.heappush(pend[ops[i].eng], (0.0, i))
        order = []
        done = 0
        while done < n:
            best = None
            for e in self.ENGS:
                pe_, re_ = pend[e], rdy[e]
                while pe_ and pe_[0][0] <= tE[e]:
                    heapq.heappush(re_, heapq.heappop(pe_)[1])
                if re_:
                    cand = (tE[e], re_[0], e, True)
                elif pe_:
                    cand = (pe_[0][0], pe_[0][1], e, False)
                else:
                    continue
                if best is None or cand[:2] < best[:2]:
                    best = cand
            st, i, e, from_ready = best
            if from_ready:
                heapq.heappop(rdy[e])
            else:
                heapq.heappop(pend[e])
            op = ops[i]
            tE[e] = st + op.issue
            finish[i] = st + op.dur
            order.append(i)
            done += 1
            for k in succs[i]:
                npred[k] -= 1
                if finish[i] > ready_t[k]:
                    ready_t[k] = finish[i]
                if npred[k] == 0:
                    heapq.heappush(pend[ops[k].eng], (ready_t[k], k))
        self.sim_time = max(finish) if n else 0.0
        return order

    def resolve(self):
        ops = self.ops
        preds = self._deps()
        if SCHEDULE:
            order = self._schedule(preds)
        else:
            order = list(range(len(ops)))
        self.order = order
        pos = [0] * len(ops)
        cntq = {e: 0 for e in self.ENGS}
        dma_cum = {}
        for i in order:
            op = ops[i]
            cntq[op.eng] += 1
            pos[i] = cntq[op.eng]
            if op.dma is not None:
                dma_cum[op.dma] = dma_cum.get(op.dma, 0) + 16
                op.cum = dma_cum[op.dma]
        known = {e: {} for e in self.ENGS}
        byq = {}
        for i in order:
            op = ops[i]
            need = {}
            for j in preds[i]:
                pj = ops[j]
                if pj.dma is not None:
                    key = ("dma", pj.dma)
                    val = pj.cum
                    ref = j
                else:
                    if pj.eng == "pe" and op.eng == "pe" and op.dma is None:
                        continue
                    key = ("eng", pj.eng)
                    val = pos[j]
                    ref = j
                if key not in need or need[key][0] < val:
                    need[key] = (val, ref)
            kn = known[op.eng]
            for key, (val, ref) in need.items():
                if kn.get(key, -1) >= val:
                    continue
                kn[key] = val
                if key[0] == "eng":
                    ops[ref].inc = True
                    op.waits.append((key, ref))
                else:
                    op.waits.append((key, val))
        cnt = {e: 0 for e in self.ENGS}
        for i in order:
            op = ops[i]
            if op.dma is None and op.inc:
                cnt[op.eng] += 1
                op.seq = cnt[op.eng]
        self.dma_tot = dma_cum
        return cnt

    def emit(self, nc, final_waits=()):
        cnt = self.resolve()
        per = {e: [] for e in self.ENGS}
        for i in self.order:
            op = self.ops[i]
            per[op.eng].append(op)
        with ExitStack() as st:
            esem = {e: st.enter_context(nc.semaphore("s_" + e)) for e in self.ENGS}
            dsem = {d: st.enter_context(nc.semaphore("d_" + d)) for d in sorted(self.dma_tot)}
            block = st.enter_context(nc.Block())

            def run(eng_name):
                def body(eng):
                    for op in per[eng_name]:
                        for key, val in op.waits:
                            if key[0] == "eng":
                                eng.wait_ge(esem[key[1]], self.ops[val].seq)
                            else:
                                eng.wait_ge(dsem[key[1]], val)
                        ins = op.fn(eng)
                        if op.dma is not None:
                            ins.then_inc(dsem[op.dma], 16)
                        elif op.inc:
                            ins.then_inc(esem[eng_name], 1)
                    if eng_name == "sp":
                        for d in final_waits:
                            eng.wait_ge(dsem[d], self.dma_tot[d])

                return body

            block.tensor(run("pe"))
            block.scalar(run("act"))
            block.vector(run("dve"))
            block.gpsimd(run("pool"))
            block.sync(run("sp"))
        return cnt


def _chunk_rows(w, kc):
    return np.ascontiguousarray(w.reshape(kc, 128, -1).transpose(1, 0, 2))


def prep_shared(inp):
    f = lambda a: np.asarray(a, dtype=np.float32)
    sh = {}
    gall = np.zeros((128, 136), np.float32)
    for l in range(DEPTH):
        for off, key in ((G_FFN1, "ffn1_norm"), (G_MIX, "mix_norm"), (G_FFN2, "ffn2_norm"), (G_MEM, "mem_norm")):
            gall[:, l * 32 + off:l * 32 + off + 8] = f(inp[key])[l].reshape(8, 128).T
    gall[:, 128:136] = f(inp["final_norm"]).reshape(8, 128).T
    sh["gall"] = gall
    wgu = np.empty((DEPTH, 2, FC, 128, 2, 8, 128), np.float32)
    wd = np.empty((DEPTH, 2, 8, 128, FC, 128), np.float32)
    for fi, pre in enumerate(("ffn1", "ffn2")):
        g = f(inp[pre + "_w_gate"]).reshape(DEPTH, 8, 128, FC, 128)
        u = f(inp[pre + "_w_up"]).reshape(DEPTH, 8, 128, FC, 128)
        wgu[:, fi, :, :, 0] = g.transpose(0, 3, 2, 1, 4)
        wgu[:, fi, :, :, 1] = u.transpose(0, 3, 2, 1, 4)
        dn = f(inp[pre + "_w_down"]).reshape(DEPTH, FC, 128, 8, 128)
        wd[:, fi] = dn.transpose(0, 3, 2, 1, 4)
    sh["wgu"] = wgu
    sh["wd"] = wd
    w_in = f(inp["w_in"]).reshape(DEPTH, 8, 128, 3584).transpose(0, 2, 1, 3)
    win_m = np.empty((DEPTH, 4, 128, 8, 384), np.float32)
    win_g = np.empty((DEPTH, 4, 128, 8, 384), np.float32)
    for g in range(4):
        win_m[:, g] = w_in[..., 384 * g:384 * g + 384]
        win_g[:, g] = w_in[..., 1536 + 384 * g:1536 + 384 * g + 384]
    sh["win_m"] = win_m
    sh["win_g"] = win_g
    sh["win_q"] = np.ascontiguousarray(w_in[..., 3072:3584])
    wkv = f(inp["w_mem_kv"]).reshape(DEPTH, 8, 128, 1024).transpose(0, 2, 1, 3)
    sh["wkv_k"] = np.ascontiguousarray(wkv[..., 0:512])
    sh["wkv_v"] = np.ascontiguousarray(wkv[..., 512:1024])
    sh["wo"] = np.ascontiguousarray(f(inp["w_out"]).reshape(DEPTH, 16, 128, 1024).transpose(0, 2, 1, 3))
    def chan(v):
        return v.reshape(12, 128).T
    lruv = np.zeros((2, 128, 12, 8), np.float32)
    mlv = np.zeros((2, 128, 12, 8), np.float32)
    for j in range(2):
        for t in range(4):
            lruv[j, :, :, t] = chan(f(inp["lru_conv_w"])[j, t])
            mlv[j, :, :, t] = chan(f(inp["ml_conv_w"])[j, t])
        lruv[j, :, :, 4] = chan(f(inp["lru_conv_b"])[j])
        lruv[j, :, :, 5] = chan(f(inp["lru_b_r"])[j])
        lruv[j, :, :, 6] = chan(f(inp["lru_b_i"])[j])
        lruv[j, :, :, 7] = chan(f(inp["lru_lambda"])[j])
        mlv[j, :, :, 4] = chan(f(inp["ml_conv_b"])[j])
        mlv[j, :, :, 5] = chan(f(inp["ml_ln_g"])[j])
        mlv[j, :, :, 6] = chan(f(inp["ml_skip"])[j])
    sh["lruv"] = lruv
    sh["mlv"] = mlv
    lrw = np.zeros((2, 4, 128, 2, 3, 384), np.float32)
    for ri, key in enumerate(("lru_w_r", "lru_w_i")):
        w = f(inp[key])
        for j in range(2):
            for n in range(8):
                g = n // 2
                dense = np.zeros((384, 384), np.float32)
                o = (n % 2) * 192
                dense[o:o + 192, o:o + 192] = w[j, n]
                if n % 2 == 1:
                    dense[0:192, 0:192] = w[j, n - 1]
                    lrw[j, g, :, ri] = dense.reshape(3, 128, 384).transpose(1, 0, 2)
    sh["lrw"] = lrw
    mlbd = np.zeros((2, 128, 3, 12, 128), np.float32)
    mlbdT = np.zeros((2, 128, 3, 12, 128), np.float32)
    for qi, key in enumerate(("ml_w_q", "ml_w_k", "ml_w_v")):
        w = f(inp[key])
        for j in range(2):
            bd = np.zeros((12, 128, 128), np.float32)
            wj = w[j].reshape(12, 32, 4, 4)
            for b in range(32):
                bd[:, 4 * b:4 * b + 4, 4 * b:4 * b + 4] = wj[:, b]
            mlbd[j, :, qi] = bd.transpose(1, 0, 2)
            mlbdT[j, :, qi] = bd.transpose(2, 0, 1)
    sh["mlbd"] = mlbd
    sh["mlbdT"] = mlbdT
    sh["mlwg"] = np.ascontiguousarray(f(inp["ml_w_gates"]).reshape(2, 36, 128, 8).transpose(0, 2, 1, 3))
    bg = np.zeros((8, 2), np.float32)
    bg[:, :] = f(inp["ml_b_gates"]).T
    sh["mlbg"] = bg
    cst = np.zeros((128, 4, 128), np.float32)
    cst[:, 0] = 1.0
    cst[:, 1] = np.eye(128, dtype=np.float32)
    cst[:, 2] = np.triu(np.ones((128, 128), np.float32))
    cst[:, 3] = np.where(np.triu(np.ones((128, 128))) > 0, 0.0, -32768.0)
    sh["cst"] = cst
    return sh


def prep_core(x_b, mem_b):
    xin = np.ascontiguousarray(np.asarray(x_b, np.float32).T.reshape(8, 128, S).transpose(1, 0, 2))
    memT = np.ascontiguousarray(np.asarray(mem_b, np.float32).T.reshape(8, 128, 256).transpose(1, 0, 2))
    return xin, memT


class Ring:
    def __init__(self, bld, name, nslots, shape, dtype, arena=False):
        self.b = bld
        self.name = name
        self.n = nslots
        self.arena = arena
        self.tiles = [bld.tile(f"{name}{i}", shape, dtype, arena) for i in range(nslots)]
        self.res = [("A:" if arena else "") + f"{name}{i}" for i in range(nslots)]
        self.cnt = 0

    def load(self, fn_src):
        s = self.cnt % self.n
        self.cnt += 1
        t = self.tiles[s]
        out_ap, in_ap = fn_src(t)
        nbytes = self.b._n(out_ap) * 4 * 128
        self.b.p.add("pool", lambda e, o=out_ap, i=in_ap: e.dma_start(out=o, in_=i),
                     writes=[self.res[s]], dma=f"{self.name}{s}", dur=2500 + nbytes / 120.0, issue=2900)
        return s


class Builder:
    ARENA_BYTES = 128 * 1024

    def __init__(self, layers, first, last):
        self.layers = layers
        self.first = first
        self.last = last
        self.nc = bass.Bass("TRN2", target_bir_lowering=False)
        self.p = Prog()
        self.st = ExitStack()
        self.ar_off = 0
        self.ar_peak = 0
        self.ps_rr = 0
        self.ps_reserved = set()
        self.uid = 0

    def dram(self, name, shape, dtype=F32, kind="ExternalInput"):
        return self.nc.dram_tensor(name, list(shape), dtype, kind=kind).ap()

    def sb(self, name, shape, dtype):
        return self.st.enter_context(self.nc.sbuf_tensor(name, list(shape), dtype))

    def tile(self, name, shape, dtype, arena):
        if not arena:
            return self.sb(name, shape, dtype)
        return self.aalloc(shape, dtype)

    def aalloc(self, shape, dtype):
        esz = 4 if dtype == F32 else 2
        n = 1
        for s_ in shape[1:]:
            n *= s_
        nbytes = (n * esz + 63) // 64 * 64
        off = self.ar_off
        self.ar_off += nbytes
        self.ar_peak = max(self.ar_peak, self.ar_off)
        assert self.ar_off <= self.ARENA_BYTES, f"arena overflow {self.ar_off}"
        raw = self.arena[0:shape[0], off // 4:(off + nbytes) // 4]
        if dtype != F32:
            raw = raw.bitcast(dtype)
        raw = raw[:, 0:n]
        if len(shape) == 2:
            return raw
        if len(shape) == 3:
            return raw.rearrange("p (a b) -> p a b", b=shape[2])
        if len(shape) == 4:
            return raw.rearrange("p (a b c) -> p a b c", b=shape[2], c=shape[3])
        raise ValueError(shape)

    def phase(self):
        self.ar_off = 0
        scr = self.bar_scr
        self.p.barrier(lambda e: e.memset(scr[:, 0:8], 0.0))

    def ps_next(self):
        for _ in range(16):
            i = self.ps_rr % 8
            self.ps_rr += 1
            if i not in self.ps_reserved:
                return i
        raise RuntimeError("no psum")

    @staticmethod
    def PS(i, qs=(0, 1, 2, 3)):
        return [f"ps{i}q{q}" for q in qs]

    @staticmethod
    def _n(ap):
        n = 1
        for s_ in ap.shape[1:]:
            n *= s_
        return n

    def mm(self, out, lhsT, rhs, start, stop, reads, writes):
        n = self._n(rhs)
        dur = max(64, n) * 0.53 + 12
        if rhs.dtype == F32:
            dur *= 4
        self.p.add("pe", lambda e: e.matmul(out, lhsT=lhsT, rhs=rhs, start=start, stop=stop),
                   reads=reads, writes=writes, dur=dur + 60, issue=dur)

    def act(self, out, in_, func, reads, writes, bias=None, scale=None):
        kw = {}
        if bias is not None:
            kw["bias"] = bias
        if scale is not None:
            kw["scale"] = scale
        dur = 200 + 0.83 * self._n(out)
        self.p.add("act", lambda e: e.activation(out=out, in_=in_, func=func, **kw), reads=reads, writes=writes, dur=dur)

    def _vdur(self, eng, out, slow=1.0):
        n = self._n(out)
        if eng == "pool":
            return 400 + 1.7 * n
        return 170 + 1.04 * n * slow

    def tt(self, eng, out, in0, in1, op, reads, writes):
        self.p.add(eng, lambda e: e.tensor_tensor(out=out, in0=in0, in1=in1, op=op), reads=reads, writes=writes,
                   dur=self._vdur(eng, out))

    def stt(self, out, in0, scalar, in1, op0, op1, reads, writes):
        self.p.add("dve", lambda e: e.scalar_tensor_tensor(out=out, in0=in0, scalar=scalar, in1=in1, op0=op0, op1=op1),
                   reads=reads, writes=writes, dur=self._vdur("dve", out))

    def ts(self, eng, out, in0, s1, s2, op0, op1, reads, writes):
        dur = self._vdur(eng, out) if eng != "pool" else 2000.0
        if s2 is None:
            self.p.add(eng, lambda e: e.tensor_scalar(out=out, in0=in0, scalar1=s1, scalar2=None, op0=op0),
                       reads=reads, writes=writes, dur=dur)
        else:
            self.p.add(eng, lambda e: e.tensor_scalar(out=out, in0=in0, scalar1=s1, scalar2=s2, op0=op0, op1=op1),
                       reads=reads, writes=writes, dur=dur)

    def copy(self, eng, out, in_, reads, writes):
        if eng == "act":
            self.act(out, in_, AF.Copy, reads, writes)
        else:
            self.p.add(eng, lambda e: e.tensor_copy(out=out, in_=in_), reads=reads, writes=writes,
                       dur=self._vdur(eng, out, 0.6) if eng != "pool" else 250 + 0.5 * self._n(out))

    def recip(self, out, in_, reads, writes):
        self.p.add("dve", lambda e: e.reciprocal(out=out, in_=in_), reads=reads, writes=writes,
                   dur=self._vdur("dve", out, 4.0))

    def dma(self, q, out, in_, reads, writes, sem):
        n = self._n(out) * (4 if out.dtype == F32 else 2) * out.shape[0]
        lat = 2500 + n / 120.0
        self.p.add(q, lambda e: e.dma_start(out=out, in_=in_), reads=reads, writes=writes, dma=sem,
                   dur=lat, issue=(2900 if q == "pool" else 700))

    def build(self):
        nc = self.nc
        L = self.layers
        self.d_hin = self.dram("hin", [128, 8, S])
        self.d_mem = self.dram("memT", [128, 8, 256])
        self.d_out = self.dram("hout", [128, 8, S], kind="ExternalOutput")
        names = dict(gall=[128, 136], wgu=[DEPTH, 2, FC, 128, 2, 8, 128], wd=[DEPTH, 2, 8, 128, FC, 128],
                     win_m=[DEPTH, 4, 128, 8, 384], win_g=[DEPTH, 4, 128, 8, 384], win_q=[DEPTH, 128, 8, 512],
                     wkv_k=[DEPTH, 128, 8, 512], wkv_v=[DEPTH, 128, 8, 512], wo=[DEPTH, 128, 16, 1024],
                     lruv=[2, 128, 12, 8], mlv=[2, 128, 12, 8], lrw=[2, 4, 128, 2, 3, 384],
                     mlbd=[2, 128, 3, 12, 128], mlbdT=[2, 128, 3, 12, 128], mlwg=[2, 128, 36, 8], mlbg=[8, 2],
                     cst=[128, 4, 128])
        self.dr = {k: self.dram(k, v) for k, v in names.items()}
        self.scr_xc = self.dram("scr_xc", [12, NT, 128, 512], F32, kind="Internal")
        self.scr_sg = self.dram("scr_sg", [12, NT, 128, 512], F32, kind="Internal")
        self.scr_xcb = self.dram("scr_xcb", [12, NT, 128, 512], BF16, kind="Internal")
        self.scr_mb = self.dram("scr_mb", [12, NT, 128, 512], BF16, kind="Internal")
        self.scr_ct = self.dram("scr_ct", [2, 4, 128, 3, 512], F32, kind="Internal")

        sb = self.sb
        self.hT = sb("hT", [128, 8, TH], F32)
        self.arena = sb("arena", [128, self.ARENA_BYTES // 4], F32)
        self.bar_scr = sb("bar_scr", [128, 8], F32)
        self.gall = sb("gall_sb", [128, 136], F32)
        self.cst = sb("cst_sb", [128, 4, 128], F32)
        self.ones_bf = sb("ones_bf", [128, 128], BF16)
        self.ident_bf = sb("ident_bf", [128, 128], BF16)
        self.mask_bf = sb("mask_bf", [128, 128], BF16)
        self.memT = sb("memT_sb", [128, 8, 256], F32)
        self.lruv = sb("lruv_sb", [128, 2, 12, 8], F32)
        self.mlv = sb("mlv_sb", [128, 2, 12, 8], F32)
        self.mlbg = sb("mlbg_sb", [8, 2], F32)
        self.nsp8 = sb("nsp8", [128, 2, 12], F32)
        self.lru_h = sb("lru_h", [128, 2, 12], F32)
        self.ctail = sb("ctail", [128, DEPTH, 12, 3], F32)
        self.epsc = sb("epsc", [128, 1], F32)
        self.ps = [self.st.enter_context(nc.psum_tensor(f"ps{i}", [128, 512], F32)) for i in range(8)]
        self.wgu = Ring(self, "wgu", 4, [128, 2, 8, 128], BF16)
        self.wdr = Ring(self, "wd", 3, [128, FC, 128], BF16)

        ones_f = self.cst[:, 0, :]
        self.ones_f, self.ident_f, self.tri_f = ones_f, self.cst[:, 1, :], self.cst[:, 2, :]

        self.dma("sp", self.gall[:], self.dr["gall"][:, :], [], ["gall"], "ld0")
        self.dma("sp", self.cst[:], self.dr["cst"][:, :, :], [], ["cst"], "ld1")
        self.dma("sp", self.memT[:], self.d_mem[:, :, :], [], ["memT"], "ld2")
        self.dma("sp", self.lruv[:], self.dr["lruv"].rearrange("j p c t -> p j c t"), [], ["lruv"], "ld3")
        self.dma("sp", self.mlv[:], self.dr["mlv"].rearrange("j p c t -> p j c t"), [], ["mlv"], "ld4")
        self.dma("sp", self.mlbg[:], self.dr["mlbg"][:, :], [], ["mlbg"], "ld5")
        self.copy("dve", self.ones_bf[:], self.cst[:, 0, :], ["cst"], ["ones_bf"])
        self.copy("dve", self.ident_bf[:], self.cst[:, 1, :], ["cst"], ["ident_bf"])
        self.copy("dve", self.mask_bf[:], self.cst[:, 3, :], ["cst"], ["mask_bf"])
        self.p.add("dve", lambda e: e.memset(self.epsc[:], EPS), writes=["epsc"])
        self.p.add("dve", lambda e: e.memset(self.lru_h[:], 0.0), writes=[f"lru_h{j}_{c}" for j in range(2) for c in range(12)])
        self.p.add("dve", lambda e: e.memset(self.ctail[:], 0.0), writes=[f"ctail{l}_{c}" for l in range(DEPTH) for c in range(12)])
        lam = self.lruv[:, :, :, 7]
        self.act(self.nsp8[:], lam, AF.Exp, ["lruv"], ["nsp8"], scale=-1.0)
        self.act(self.nsp8[:], self.nsp8[:], AF.Ln, ["nsp8"], ["nsp8"], bias=1.0)
        self.p.add("act", lambda e: e.mul(self.nsp8[:], self.nsp8[:], -8.0), reads=["nsp8"], writes=["nsp8"])

        self.wgu_plan = []
        self.wd_plan = []
        for hf in range(2):
            for l in L:
                for fi in range(2):
                    for fc in range(FC):
                        self.wgu_plan.append((l, fi, fc))
                    for mc in range(8):
                        self.wd_plan.append((l, fi, mc))
        self.wgu_issued = 0
        self.wd_issued = 0
        self.wgu_used = 0
        self.wd_used = 0

        for hf in range(2):
            t0 = hf * TH
            self.dma("sp", self.hT[:], self.d_hin[:, :, t0:t0 + TH], [], self.hT_res(), "ldh")
            for l in L:
                self.ffn(l, 0, hf)
                self.mixer(l, hf)
                self.ffn(l, 1, hf)
            self.finish(hf)
        cnt = self.p.emit(nc, final_waits=["out0", "out1"])
        self.st.close()
        return nc, cnt

    def hT_res(self, kcs=range(8), tts=range(NT)):
        return [f"hT{kc}_{tt}" for kc in kcs for tt in tts]

    def wgu_prefetch(self, upto):
        while self.wgu_issued < min(upto, len(self.wgu_plan)):
            l, fi, fc = self.wgu_plan[self.wgu_issued]
            src = self.dr["wgu"][l, fi, fc]
            self.wgu.load(lambda t, src=src: (t[:], src))
            self.wgu_issued += 1

    def wd_prefetch(self, upto):
        while self.wd_issued < min(upto, len(self.wd_plan)):
            l, fi, mc = self.wd_plan[self.wd_issued]
            src = self.dr["wd"][l, fi, mc]
            self.wdr.load(lambda t, src=src: (t[:], src))
            self.wd_issued += 1

    def rmsnorm(self, src, src_res_fn, gcol0, dst, dst_res_fn, ntok, sq, rs):
        ntile = (ntok + 511) // 512
        for tt in range(ntile):
            w = min(512, ntok - tt * 512)
            sl = slice(tt * 512, tt * 512 + w)
            self.act(sq[:, :, 0:w], src[:, :, sl], AF.Square, src_res_fn(range(8), [tt]), ["A:sq"])
            pb = self.ps_next()
            for kc in range(8):
                self.mm(self.ps[pb][:, 0:w], self.ones_bf[:], sq[:, kc, 0:w], kc == 0, kc == 7,
                        ["A:sq", "ones_bf"], self.PS(pb))
            self.act(rs[:, 0:w], self.ps[pb][:, 0:w], AF.Sqrt, self.PS(pb) + ["epsc"], ["A:rs"],
                     bias=self.epsc[:, 0:1], scale=1.0 / D)
            self.recip(rs[:, 0:w], rs[:, 0:w], ["A:rs"], ["A:rs"])
            for kc in range(8):
                self.stt(dst[:, kc, sl], src[:, kc, sl], self.gall[:, gcol0 + kc:gcol0 + kc + 1], rs[:, 0:w],
                         ALU.mult, ALU.mult, src_res_fn([kc], [tt]) + ["A:rs", "gall"], dst_res_fn(kc, tt))

    def ffn(self, l, fi, hf):
        self.phase()
        xn = self.aalloc([128, 8, TH], BF16)
        hid = self.aalloc([128, FC, TH], BF16)
        sq = self.aalloc([128, 8, 512], BF16)
        rs = self.aalloc([128, 512], F32)
        sg = [self.aalloc([128, 512], F32) for _ in range(2)]
        gcol = l * 32 + (G_FFN1 if fi == 0 else G_FFN2)
        self.wgu_prefetch(self.wgu_used + 3)
        self.rmsnorm(self.hT, lambda kcs, tts: self.hT_res(kcs, tts), gcol, xn,
                     lambda kc, tt: [f"A:xn{kc}_{tt}"], TH, sq, rs)
        xn_all = [f"A:xn{kc}_{tt}" for kc in range(8) for tt in range(NT)]
        sgc = 0
        for fc in range(FC):
            s = self.wgu_used % 4
            self.wgu_used += 1
            self.wgu_prefetch(self.wgu_used + 3)
            if fc == FC - 3:
                self.wd_prefetch(self.wd_used + 2)
            wt = self.wgu.tiles[s]
            banks = [[self.ps_next() for _ in range(NT)] for _ in range(2)]
            for gu in range(2):
                for kc in range(8):
                    for tt in range(NT):
                        self.mm(self.ps[banks[gu][tt]][:], wt[:, gu, kc, :], xn[:, kc, tt * 512:(tt + 1) * 512],
                                kc == 0, kc == 7, [self.wgu.res[s], f"A:xn{kc}_{tt}"], self.PS(banks[gu][tt]))
            for tt in range(NT):
                sgt = sg[sgc % 2]
                sgr = f"A:sg{sgc % 2}"
                sgc += 1
                self.act(sgt[:], self.ps[banks[0][tt]][:], AF.Silu, self.PS(banks[0][tt]), [sgr])
                self.tt("dve", hid[:, fc, tt * 512:(tt + 1) * 512], self.ps[banks[1][tt]][:], sgt[:], ALU.mult,
                        self.PS(banks[1][tt]) + [sgr], [f"A:hid{fc}_{tt}"])
        for mc in range(8):
            s = self.wd_used % 3
            self.wd_used += 1
            self.wd_prefetch(self.wd_used + 2)
            wt = self.wdr.tiles[s]
            banks = [self.ps_next() for _ in range(NT)]
            for fc in range(FC):
                for tt in range(NT):
                    self.mm(self.ps[banks[tt]][:], wt[:, fc, :], hid[:, fc, tt * 512:(tt + 1) * 512],
                            fc == 0, fc == FC - 1, [self.wdr.res[s], f"A:hid{fc}_{tt}"], self.PS(banks[tt]))
            for tt in range(NT):
                sl = slice(tt * 512, (tt + 1) * 512)
                self.stt(self.hT[:, mc, sl], self.ps[banks[tt]][:], 0.5, self.hT[:, mc, sl], ALU.mult, ALU.add,
                         self.PS(banks[tt]) + [f"hT{mc}_{tt}"], [f"hT{mc}_{tt}"])

    def outproj(self, ysrc, yres, nk, wo_t, wo_res, tt):
        sl = slice(tt * 512, (tt + 1) * 512)
        for mc in range(8):
            pb = self.ps_next()
            for k in range(nk):
                self.mm(self.ps[pb][:], wo_t[:, k, mc * 128:(mc + 1) * 128], ysrc[:, k, :], k == 0, k == nk - 1,
                        [wo_res, yres], self.PS(pb))
            self.tt("dve", self.hT[:, mc, sl], self.ps[pb][:], self.hT[:, mc, sl], ALU.add,
                    self.PS(pb) + [f"hT{mc}_{tt}"], [f"hT{mc}_{tt}"])

    def mixer(self, l, hf):
        self.phase()
        j = l // 2
        is_lru = (l % 2 == 0)
        self.wor = Ring(self, "wo", 2, [128, 4, 1024], BF16, arena=True)
        if not is_lru:
            self.ml_alloc()
        xn = self.aalloc([128, 8, TH], BF16)
        sq = self.aalloc([128, 8, 512], BF16)
        rs = self.aalloc([128, 512], F32)
        self.win = Ring(self, "win", 3, [128, 8, 512], BF16, arena=True)
        self.rmsnorm(self.hT, lambda kcs, tts: self.hT_res(kcs, tts), l * 32 + G_MIX, xn,
                     lambda kc, tt: [f"A:xn{kc}_{tt}"], TH, sq, rs)
        self.xn = xn
        off_x = self.ar_off
        self.xattn(l, sq, rs)
        self.ar_off = off_x
        scr = self.bar_scr
        self.p.barrier(lambda e: e.memset(scr[:, 0:8], 0.0))
        if is_lru:
            self.lru(l, j, hf)
        else:
            self.mlstm(l, j, hf)

    def xattn(self, l, sq, rs):
        memn = self.aalloc([128, 8, 256], BF16)
        kT = self.aalloc([128, 4, 256], BF16)
        vm = self.aalloc([128, 2, 512], BF16)
        qh = [self.aalloc([128, 512], BF16) for _ in range(2)]
        ex = [self.aalloc([128, 2, 512], BF16) for _ in range(2)]
        rden = self.aalloc([128, 512], F32)
        xa = [self.aalloc([128, 4, 512], BF16)]
        sk = self.win.load(lambda t: (t[:], self.dr["wkv_k"][l]))
        wkv, wkvr = self.win.tiles[sk], self.win.res[sk]
        sq_ = self.win.load(lambda t: (t[:], self.dr["win_q"][l]))
        winq, winqr = self.win.tiles[sq_], self.win.res[sq_]
        ws = self.wor.load(lambda t: (t[:], self.dr["wo"][l, :, 12:16, :]))
        self.rmsnorm(self.memT, lambda kcs, tts: ["memT"], l * 32 + G_MEM, memn,
                     lambda kc, tt: ["A:memn"], 256, sq, rs)
        for h in range(4):
            pb = self.ps_next()
            for kc in range(8):
                self.mm(self.ps[pb][:, 0:256], wkv[:, kc, h * 128:(h + 1) * 128], memn[:, kc, :], kc == 0, kc == 7,
                        [wkvr, "A:memn"], self.PS(pb))
            self.copy("act", kT[:, h, :], self.ps[pb][:, 0:256], self.PS(pb), ["A:kT"])
        sv = self.win.load(lambda t: (t[:], self.dr["wkv_v"][l]))
        wkv, wkvr = self.win.tiles[sv], self.win.res[sv]
        for mcm in range(2):
            pb = self.ps_next()
            for kc in range(8):
                self.mm(self.ps[pb][:], memn[:, kc, mcm * 128:(mcm + 1) * 128], wkv[:, kc, :], kc == 0, kc == 7,
                        [wkvr, "A:memn"], self.PS(pb))
            self.copy("act", vm[:, mcm, :], self.ps[pb][:], self.PS(pb), ["A:vm"])
        c = 0
        for tt in range(NT):
            sl = slice(tt * 512, (tt + 1) * 512)
            xat = xa[0]
            xar = "A:xa0"
            for h in range(4):
                q_t, q_r = qh[c % 2], f"A:qh{c % 2}"
                e_t, e_r = ex[c % 2], f"A:ex{c % 2}"
                c += 1
                pb = self.ps_next()
                for kc in range(8):
                    self.mm(self.ps[pb][:], winq[:, kc, h * 128:(h + 1) * 128], self.xn[:, kc, sl], kc == 0, kc == 7,
                            [winqr, f"A:xn{kc}_{tt}"], self.PS(pb))
                self.copy("act", q_t[:], self.ps[pb][:], self.PS(pb), [q_r])
                for mcm in range(2):
                    pb = self.ps_next()
                    self.mm(self.ps[pb][:], kT[:, h, mcm * 128:(mcm + 1) * 128], q_t[:], True, True,
                            ["A:kT", q_r], self.PS(pb))
                    self.act(e_t[:, mcm, :], self.ps[pb][:], AF.Exp, self.PS(pb), [e_r + f"_{mcm}"], scale=XA_SCALE)
                po = self.ps_next()
                pd = self.ps_next()
                for mcm in range(2):
                    self.mm(self.ps[po][:], vm[:, mcm, h * 128:(h + 1) * 128], e_t[:, mcm, :], mcm == 0, mcm == 1,
                            ["A:vm", e_r + f"_{mcm}"], self.PS(po))
                for mcm in range(2):
                    self.mm(self.ps[pd][:], self.ones_bf[:], e_t[:, mcm, :], mcm == 0, mcm == 1,
                            ["ones_bf", e_r + f"_{mcm}"], self.PS(pd))
                self.recip(rden[:], self.ps[pd][:], self.PS(pd), ["A:rden"])
                self.tt("dve", xat[:, h, :], self.ps[po][:], rden[:], ALU.mult, self.PS(po) + ["A:rden"], [xar])
            self.outproj(xat, xar, 4, self.wor.tiles[ws], self.wor.res[ws], tt)

    def conv(self, l, cg, vec, mainp, mres, xc, xres):
        w = lambda t: vec[:, cg, t:t + 1]
        self.copy("pool", mainp[:, 0:3], self.ctail[:, l, cg, :], [f"ctail{l}_{cg}"], [mres + "h"])
        self.ts("dve", xc, mainp[:, 0:512], w(0), w(4), ALU.mult, ALU.add, [mres, mres + "h"], [xres])
        for t in range(1, 4):
            self.stt(xc, mainp[:, t:t + 512], w(t), xc, ALU.mult, ALU.add, [mres, mres + "h", xres], [xres])
        self.copy("pool", self.ctail[:, l, cg, :], mainp[:, 512:515], [mres], [f"ctail{l}_{cg}"])

    def main_proj(self, wm, wm_res, c, tt, mainp, mres):
        sl = slice(tt * 512, (tt + 1) * 512)
        pb = self.ps_next()
        for kc in range(8):
            self.mm(self.ps[pb][:], wm[:, kc, c * 128:(c + 1) * 128], self.xn[:, kc, sl], kc == 0, kc == 7,
                    [wm_res, f"A:xn{kc}_{tt}"], self.PS(pb))
        self.copy("act", mainp[:, 3:515], self.ps[pb][:], self.PS(pb), [mres])

    def lru(self, l, j, hf):
        vec = self.lruv[:, j]
        lrw = Ring(self, "lrw", 2, [128, 2, 3, 384], BF16, arena=True)
        mainp = [self.aalloc([128, 515], F32) for _ in range(3)]
        xc = [self.aalloc([128, 512], F32) for _ in range(3)]
        xcb = [self.aalloc([128, 512], BF16) for _ in range(3)]
        A = [self.aalloc([128, 512], F32) for _ in range(2)]
        T1 = [self.aalloc([128, 512], F32) for _ in range(2)]
        T2 = [self.aalloc([128, 512], F32) for _ in range(2)]
        HH = [self.aalloc([128, 512], F32) for _ in range(2)]
        GG = [self.aalloc([128, 512], F32) for _ in range(2)]
        yb = [self.aalloc([128, 3, 512], BF16) for _ in range(2)]
        it = 0
        ic = 0
        for g in range(4):
            sm = self.win.load(lambda t: (t[:, :, 0:384], self.dr["win_m"][l, g]))
            sg_ = self.win.load(lambda t: (t[:, :, 0:384], self.dr["win_g"][l, g]))
            sw = lrw.load(lambda t: (t[:], self.dr["lrw"][j, g]))
            so = self.wor.load(lambda t: (t[:, 0:3, :], self.dr["wo"][l, :, 3 * g:3 * g + 3, :]))
            wm, wg = self.win.tiles[sm], self.win.tiles[sg_]
            wmr, wgr = self.win.res[sm], self.win.res[sg_]
            wl, wlr = lrw.tiles[sw], lrw.res[sw]
            for tt in range(NT):
                sl = slice(tt * 512, (tt + 1) * 512)
                y_t, y_r = yb[it % 2], f"A:y{it % 2}"
                it += 1
                for c in range(3):
                    cg = 3 * g + c
                    self.main_proj(wm, wmr, c, tt, mainp[c], f"A:mp{c}")
                    self.conv(l, cg, vec, mainp[c], f"A:mp{c}", xc[c][:], f"A:xc{c}")
                    self.copy("pool", xcb[c][:], xc[c][:], [f"A:xc{c}"], [f"A:xcb{c}"])
                for c in range(3):
                    cg = 3 * g + c
                    k = ic % 2
                    ic += 1
                    a_t, t1, t2, hh, gg = A[k], T1[k], T2[k], HH[k], GG[k]
                    ar, t1r, t2r, hr, gr = f"A:a{k}", f"A:t1{k}", f"A:t2{k}", f"A:hh{k}", f"A:gg{k}"
                    pb = self.ps_next()
                    for kk in range(3):
                        self.mm(self.ps[pb][:], wl[:, 0, kk, c * 128:(c + 1) * 128], xcb[kk][:], kk == 0, kk == 2,
                                [wlr, f"A:xcb{kk}"], self.PS(pb))
                    self.act(a_t[:], self.ps[pb][:], AF.Sigmoid, self.PS(pb) + ["lruv"], [ar], bias=vec[:, cg, 5:6])
                    self.act(a_t[:], a_t[:], AF.Exp, [ar, "nsp8"], [ar], scale=self.nsp8[:, j, cg:cg + 1])
                    pb = self.ps_next()
                    for kk in range(3):
                        self.mm(self.ps[pb][:], wl[:, 1, kk, c * 128:(c + 1) * 128], xcb[kk][:], kk == 0, kk == 2,
                                [wlr, f"A:xcb{kk}"], self.PS(pb))
                    self.act(t2[:], self.ps[pb][:], AF.Sigmoid, self.PS(pb) + ["lruv"], [t2r], bias=vec[:, cg, 6:7])
                    self.act(t1[:], a_t[:], AF.Square, [ar], [t1r])
                    self.act(t1[:], t1[:], AF.Sqrt, [t1r], [t1r], bias=1.0, scale=-1.0)
                    self.tt("dve", t2[:], t2[:], xc[c][:], ALU.mult, [t2r, f"A:xc{c}"], [t2r])
                    self.tt("dve", t2[:], t2[:], t1[:], ALU.mult, [t2r, t1r], [t2r])
                    hst = self.lru_h[:, j, cg:cg + 1]
                    self.p.add("dve", lambda e, hh=hh, a_t=a_t, t2=t2, hst=hst: e.tensor_tensor_scan(
                        out=hh[:], data0=a_t[:], data1=t2[:], initial=hst, op0=ALU.mult, op1=ALU.add),
                        reads=[ar, t2r, f"lru_h{j}_{cg}"], writes=[hr], dur=1300.0)
                    self.copy("pool", hst, hh[:, 511:512], [hr], [f"lru_h{j}_{cg}"])
                    pb = self.ps_next()
                    for kc in range(8):
                        self.mm(self.ps[pb][:], wg[:, kc, c * 128:(c + 1) * 128], self.xn[:, kc, sl], kc == 0, kc == 7,
                                [wgr, f"A:xn{kc}_{tt}"], self.PS(pb))
                    self.act(gg[:], self.ps[pb][:], AF.Gelu_apprx_tanh, self.PS(pb), [gr])
                    self.tt("dve", y_t[:, c, :], hh[:], gg[:], ALU.mult, [hr, gr], [y_r])
                self.outproj(y_t, y_r, 3, self.wor.tiles[so], self.wor.res[so], tt)

    def ml_alloc(self):
        bd = self.aalloc([128, 3, 12, 128], BF16)
        wgs = self.aalloc([128, 36, 8], BF16)
        wc = self.aalloc([128, 24, 8], BF16)
        gT = self.aalloc([8, TH], F32)
        gtm = self.aalloc([128, 8, 8], F32)
        logf = self.aalloc([128, 8, 4], F32)
        bias_s = self.aalloc([128, 8, 4], F32)
        wS = self.aalloc([128, 8, 4], F32)
        eB = self.aalloc([128, 8, 4], F32)
        self.ml_t = (bd, wgs, wc, gT, gtm, logf, bias_s, wS, eB)
        self.ml_off = self.ar_off

    def mlstm(self, l, j, hf):
        vec = self.mlv[:, j]
        bd, wgs, wc, gT, gtm, logf, bias_s, wS, eB = self.ml_t
        off_pass = self.ml_off
        bdT = self.aalloc([128, 3, 12, 128], BF16)
        self.dma("pool", bd[:], self.dr["mlbd"][j], [], ["A:bd"], "bd")
        self.dma("pool", bdT[:], self.dr["mlbdT"][j], [], ["A:bdT"], "bdT")
        self.dma("pool", wgs[:], self.dr["mlwg"][j], [], ["A:wgs"], "wgs")
        pb = self.ps_next()
        for c in range(12):
            o = self.ps[pb][:, c * 8:(c + 1) * 8]
            self.mm(o, bdT[:, 0, c, :], wgs[:, c, :], True, False, ["A:bdT", "A:wgs"], self.PS(pb))
            self.mm(o, bdT[:, 1, c, :], wgs[:, 12 + c, :], False, True, ["A:bdT", "A:wgs"], self.PS(pb))
        for c in range(12):
            o = self.ps[pb][:, (12 + c) * 8:(13 + c) * 8]
            self.mm(o, bdT[:, 2, c, :], wgs[:, 24 + c, :], True, True, ["A:bdT", "A:wgs"], self.PS(pb))
        self.copy("act", wc[:].rearrange("p a b -> p (a b)"), self.ps[pb][:, 0:192], self.PS(pb), ["A:wc"])
        mainp = [self.aalloc([128, 515], F32) for _ in range(3)]
        xc = [self.aalloc([128, 512], F32) for _ in range(3)]
        xcb = [self.aalloc([128, 512], BF16) for _ in range(3)]
        mb = [self.aalloc([128, 512], BF16) for _ in range(3)]
        sgt = [self.aalloc([128, 512], F32) for _ in range(2)]
        gbank = [self.ps_next() for _ in range(NT)]
        for b_ in gbank:
            self.ps_reserved.add(b_)
        ib = 0
        isg = 0
        for g in range(4):
            sm = self.win.load(lambda t: (t[:, :, 0:384], self.dr["win_m"][l, g]))
            sg_ = self.win.load(lambda t: (t[:, :, 0:384], self.dr["win_g"][l, g]))
            wm, wg = self.win.tiles[sm], self.win.tiles[sg_]
            wmr, wgr = self.win.res[sm], self.win.res[sg_]
            for tt in range(NT):
                sl = slice(tt * 512, (tt + 1) * 512)
                for c in range(3):
                    cg = 3 * g + c
                    k = ib % 3
                    ib += 1
                    xcr, xcbr, mbr = f"A:xc{k}", f"A:xcb{k}", f"A:mb{k}"
                    self.main_proj(wm, wmr, c, tt, mainp[c], f"A:mp{c}")
                    self.conv(l, cg, vec, mainp[c], f"A:mp{c}", xc[k][:], xcr)
                    self.act(xc[k][:], xc[k][:], AF.Silu, [xcr], [xcr])
                    self.copy("pool", xcb[k][:], xc[k][:], [xcr], [xcbr])
                    self.copy("pool", mb[k][:], mainp[c][:, 3:515], [f"A:mp{c}"], [mbr])
                    first = (g == 0 and c == 0)
                    lastm = (g == 3 and c == 2)
                    self.mm(self.ps[gbank[tt]][0:8, :], wc[:, cg, :], xcb[k][:], first, False,
                            ["A:wc", xcbr], self.PS(gbank[tt]))
                    self.mm(self.ps[gbank[tt]][0:8, :], wc[:, 12 + cg, :], mb[k][:], False, lastm,
                            ["A:wc", mbr], self.PS(gbank[tt]))
                    self.dma("sp", self.scr_xc[cg, tt], xc[k][:], [xcr], [f"D:xc{cg}_{tt}"], f"sxc{k}")
                    self.dma("sp", self.scr_xcb[cg, tt], xcb[k][:], [xcbr], [f"D:xcb{cg}_{tt}"], f"sxb{k}")
                    self.dma("sp", self.scr_mb[cg, tt], mb[k][:], [mbr], [f"D:mb{cg}_{tt}"], f"smb{k}")
                    ks = isg % 2
                    isg += 1
                    pb = self.ps_next()
                    for kc in range(8):
                        self.mm(self.ps[pb][:], wg[:, kc, c * 128:(c + 1) * 128], self.xn[:, kc, sl], kc == 0, kc == 7,
                                [wgr, f"A:xn{kc}_{tt}"], self.PS(pb))
                    self.act(sgt[ks][:], self.ps[pb][:], AF.Silu, self.PS(pb), [f"A:sgt{ks}"])
                    self.dma("sp", self.scr_sg[cg, tt], sgt[ks][:], [f"A:sgt{ks}"], [f"D:sg{cg}_{tt}"], f"ssg{ks}")
        for tt in range(NT):
            self.act(gT[:, tt * 512:(tt + 1) * 512], self.ps[gbank[tt]][0:8, :], AF.Identity,
                     self.PS(gbank[tt]) + ["mlbg"], ["A:gT"], bias=self.mlbg[:, j:j + 1])
            self.ps_reserved.discard(gbank[tt])
        pb = self.ps_next()
        for jc in range(8):
            self.p.add("pe", lambda e, jc=jc, pb=pb: e.transpose(
                out=self.ps[pb][:, jc * 8:(jc + 1) * 8], in_=gT[0:8, jc * 128:(jc + 1) * 128],
                identity=self.ident_f[0:8, 0:8]), reads=["A:gT", "cst"], writes=self.PS(pb), dur=300.0)
        self.copy("act", gtm[:].rearrange("p a b -> p (a b)"), self.ps[pb][:, 0:64], self.PS(pb), ["A:gtm"])
        self.act(logf[:], gtm[:, :, 4:8], AF.Exp, ["A:gtm"], ["A:logf"], scale=-1.0)
        self.act(logf[:], logf[:], AF.Ln, ["A:logf"], ["A:logf"], bias=1.0)
        self.p.add("act", lambda e: e.mul(logf[:], logf[:], -1.0), reads=["A:logf"], writes=["A:logf"])
        lf2 = logf[:].rearrange("p a b -> p (a b)")
        pb = self.ps_next()
        self.mm(self.ps[pb][:, 0:32], self.tri_f, lf2, True, True, ["cst", "A:logf"], self.PS(pb))
        self.mm(self.ps[pb][:, 32:64], self.ones_f, lf2, True, True, ["cst", "A:logf"], self.PS(pb))
        bs2 = bias_s[:].rearrange("p a b -> p (a b)")
        self.stt(bias_s[:], gtm[:, :, 0:4], LN_SCALE, self.ps[pb][:, 0:32].rearrange("p (a b) -> p a b", b=4),
                 ALU.add, ALU.subtract, ["A:gtm"] + self.PS(pb), ["A:bias_s"])
        self.tt("dve", wS[:].rearrange("p a b -> p (a b)"), self.ps[pb][:, 32:64], bs2, ALU.add,
                ["A:bias_s"] + self.PS(pb), ["A:wS"])
        self.act(wS[:], wS[:], AF.Exp, ["A:wS"], ["A:wS"])
        self.act(eB[:].rearrange("p a b -> p (a b)"), self.ps[pb][:, 32:64], AF.Exp, self.PS(pb), ["A:eB"])
        self.ar_off = off_pass
        scr = self.bar_scr
        self.p.barrier(lambda e: e.memset(scr[:, 0:8], 0.0))
        xcl = [self.aalloc([128, 3, 512], F32) for _ in range(2)]
        sgl = [self.aalloc([128, 3, 512], F32) for _ in range(2)]
        xbl = [self.aalloc([128, 3, 512], BF16) for _ in range(2)]
        mbl = [self.aalloc([128, 3, 512], BF16) for _ in range(2)]
        qT = self.aalloc([128, 3, 512], BF16)
        kTt = self.aalloc([128, 3, 512], BF16)
        vext = self.aalloc([128, 4, 512], BF16)
        hh = self.aalloc([128, 3, 512], F32)
        hb = self.aalloc([128, 3, 512], BF16)
        hsq = self.aalloc([128, 3, 512], BF16)
        mean = self.aalloc([128, 512], F32)
        msq = self.aalloc([128, 512], F32)
        rstd = self.aalloc([128, 512], F32)
        tmp = [self.aalloc([128, 512], F32) for _ in range(2)]
        yb = [self.aalloc([128, 3, 512], BF16) for _ in range(2)]
        triL = [self.aalloc([128, 128], F32) for _ in range(2)]
        ebr = [self.aalloc([128, 128], F32) for _ in range(2)]
        Dm = [self.aalloc([128, 128], F32) for _ in range(2)]
        ST = [self.aalloc([128, 128], BF16) for _ in range(2)]
        qs = [self.aalloc([128, 3, 128], BF16) for _ in range(2)]
        kw = [self.aalloc([128, 384], BF16) for _ in range(2)]
        dd = [self.aalloc([128, 128], F32) for _ in range(2)]
        ctf = self.aalloc([128, 3, 512], F32)
        ctb = self.aalloc([128, 3, 512], BF16)
        self.p.add("pool", lambda e: e.memset(vext[:, :, 384:512], 1.0), writes=["A:vext_one"])
        it = 0
        istep = 0
        for g in range(4):
            so = self.wor.load(lambda t: (t[:, 0:3, :], self.dr["wo"][l, :, 3 * g:3 * g + 3, :]))
            if hf == 1:
                self.dma("sp", ctf[:], self.scr_ct[j, g], [f"D:ct{j}_{g}"], ["A:ctf"], "lct")
                self.copy("pool", ctb[:], ctf[:], ["A:ctf"], ["A:ctb"])
            for tt in range(NT):
                k2 = it % 2
                it += 1
                y_t, y_r = yb[k2], f"A:y{k2}"
                cgs = [3 * g + c for c in range(3)]
                c0 = cgs[0]
                for buf, dsc, nm in ((xcl, self.scr_xc, "xc"), (sgl, self.scr_sg, "sg"), (xbl, self.scr_xcb, "xcb"), (mbl, self.scr_mb, "mb")):
                    lnm = {"xc": "xcl", "sg": "sgl", "xcb": "xbl", "mb": "mbl"}[nm]
                    self.dma("sp", buf[k2][:], dsc[c0:c0 + 3, tt].rearrange("c p t -> p c t"),
                             [f"D:{nm}{cg}_{tt}" for cg in cgs], [f"A:{lnm}{k2}_{c}" for c in range(3)], f"l{nm}{k2}")
                for c in range(3):
                    cg = cgs[c]
                    pb = self.ps_next()
                    self.mm(self.ps[pb][:], bd[:, 0, cg, :], xbl[k2][:, c, :], True, True, ["A:bd", f"A:xbl{k2}_{c}"], self.PS(pb))
                    self.copy("act", qT[:, c, :], self.ps[pb][:], self.PS(pb), [f"A:qT{c}"])
                    pb = self.ps_next()
                    self.mm(self.ps[pb][:], bd[:, 1, cg, :], xbl[k2][:, c, :], True, True, ["A:bd", f"A:xbl{k2}_{c}"], self.PS(pb))
                    self.copy("act", kTt[:, c, :], self.ps[pb][:], self.PS(pb), [f"A:kT{c}"])
                for jc in range(4):
                    pb = self.ps_next()
                    for c in range(3):
                        self.mm(self.ps[pb][:, c * 128:(c + 1) * 128], mbl[k2][:, c, jc * 128:(jc + 1) * 128],
                                bd[:, 2, cgs[c], :], True, True, ["A:bd", f"A:mbl{k2}_{c}"], self.PS(pb))
                    self.copy("act", vext[:, jc, 0:384], self.ps[pb][:, 0:384], self.PS(pb), [f"A:vext{jc}"])
                for jc in range(4):
                    jcg = tt * 4 + jc
                    first_chunk = (hf == 0 and jcg == 0)
                    last_chunk = (hf == 1 and jcg == 7)
                    ks = istep % 2
                    istep += 1
                    cs = slice(jc * 128, (jc + 1) * 128)
                    colp = lambda t_: t_[:, jcg, g:g + 1]
                    pk = self.ps_next()
                    for c in range(3):
                        self.mm(self.ps[pk][:, c * 128:(c + 1) * 128], xbl[k2][:, c, cs], bd[:, 1, cgs[c], :], True, True,
                                ["A:bd", f"A:xbl{k2}_{c}"], self.PS(pk))
                    self.act(kw[ks][:], self.ps[pk][:, 0:384], AF.Identity, self.PS(pk) + ["A:wS"], [f"A:kw{ks}"],
                             scale=colp(wS))
                    self.ts("pool", triL[ks][:], self.tri_f, colp(logf), None, ALU.mult, None, ["cst", "A:logf"], [f"A:triL{ks}"])
                    pa = self.ps_next()
                    self.mm(self.ps[pa][:, 0:128], self.ones_f, triL[ks][:], True, True, ["cst", f"A:triL{ks}"], self.PS(pa))
                    self.mm(self.ps[pa][:, 128:256], self.ones_f, triL[ks][:], True, False, ["cst", f"A:triL{ks}"], self.PS(pa))
                    self.mm(self.ps[pa][:, 128:256], self.ident_bf[:], self.mask_bf[:], False, True,
                            ["ident_bf", "mask_bf"], self.PS(pa))
                    for c in range(3):
                        self.mm(self.ps[pa][:, 256:384], kTt[:, c, cs], qT[:, c, cs], c == 0, c == 2,
                                [f"A:kT{c}", f"A:qT{c}"], self.PS(pa))
                    self.act(ebr[ks][:], self.ps[pa][:, 0:128], AF.Exp, self.PS(pa), [f"A:ebr{ks}"])
                    self.act(Dm[ks][:], self.ps[pa][:, 128:256], AF.Exp, self.PS(pa) + ["A:bias_s"], [f"A:Dm{ks}"],
                             bias=colp(bias_s))
                    self.tt("dve", ST[ks][:], self.ps[pa][:, 256:384], Dm[ks][:], ALU.mult, self.PS(pa) + [f"A:Dm{ks}"], [f"A:ST{ks}"])
                    if not first_chunk:
                        for c in range(3):
                            self.tt("dve", qs[ks][:, c, :], qT[:, c, cs], ebr[ks][:], ALU.mult,
                                    [f"A:qT{c}", f"A:ebr{ks}"], [f"A:qs{ks}"])
                    pn = self.ps_next()
                    for cc in range(4):
                        o = self.ps[pn][:, cc * 128:(cc + 1) * 128]
                        vres = f"A:vext{jc}" if cc < 3 else "A:vext_one"
                        self.mm(o, vext[:, jc, cc * 128:(cc + 1) * 128], ST[ks][:], True, first_chunk,
                                [vres, f"A:ST{ks}"], self.PS(pn))
                        if not first_chunk:
                            for dk in range(3):
                                self.mm(o, ctb[:, dk, cc * 128:(cc + 1) * 128], qs[ks][:, dk, :], False, dk == 2,
                                        ["A:ctb", f"A:qs{ks}"], self.PS(pn))
                    self.act(dd[ks][:], self.ps[pn][:, 384:512], AF.Abs, self.PS(pn), [f"A:dd{ks}"])
                    self.ts("dve", dd[ks][:], dd[ks][:], 1.0, None, ALU.max, None, [f"A:dd{ks}"], [f"A:dd{ks}"])
                    self.recip(dd[ks][:], dd[ks][:], [f"A:dd{ks}"], [f"A:dd{ks}"])
                    for c in range(3):
                        self.tt("dve", hh[:, c, cs], self.ps[pn][:, c * 128:(c + 1) * 128], dd[ks][:], ALU.mult,
                                self.PS(pn) + [f"A:dd{ks}"], [f"A:hh{c}"])
                    if not last_chunk:
                        for dk in range(3):
                            pc = self.ps_next()
                            self.mm(self.ps[pc][:], kw[ks][:, dk * 128:(dk + 1) * 128], vext[:, jc, :], True, True,
                                    [f"A:kw{ks}", f"A:vext{jc}", "A:vext_one"], self.PS(pc))
                            if first_chunk:
                                self.copy("dve", ctf[:, dk, :], self.ps[pc][:], self.PS(pc), ["A:ctf"])
                            else:
                                self.stt(ctf[:, dk, :], ctf[:, dk, :], colp(eB), self.ps[pc][:], ALU.mult, ALU.add,
                                         ["A:ctf", "A:eB"] + self.PS(pc), ["A:ctf"])
                        self.copy("pool", ctb[:], ctf[:], ["A:ctf"], ["A:ctb"])
                hres = [f"A:hh{c}" for c in range(3)]
                self.copy("act", hb[:], hh[:], hres, ["A:hb"])
                self.act(hsq[:], hh[:], AF.Square, hres, ["A:hsq"])
                pm = self.ps_next()
                for c in range(3):
                    self.mm(self.ps[pm][:], self.ones_bf[:], hb[:, c, :], c == 0, c == 2, ["ones_bf", "A:hb"], self.PS(pm))
                p2 = self.ps_next()
                for c in range(3):
                    self.mm(self.ps[p2][:], self.ones_bf[:], hsq[:, c, :], c == 0, c == 2, ["ones_bf", "A:hsq"], self.PS(p2))
                self.p.add("act", lambda e, pm=pm: e.mul(mean[:], self.ps[pm][:], 1.0 / 384.0), reads=self.PS(pm), writes=["A:mean"])
                self.act(msq[:], mean[:], AF.Square, ["A:mean"], ["A:msq"])
                self.stt(rstd[:], self.ps[p2][:], 1.0 / 384.0, msq[:], ALU.mult, ALU.subtract, self.PS(p2) + ["A:msq"], ["A:rstd"])
                self.act(rstd[:], rstd[:], AF.Sqrt, ["A:rstd", "epsc"], ["A:rstd"], bias=self.epsc[:, 0:1])
                self.recip(rstd[:], rstd[:], ["A:rstd"], ["A:rstd"])
                for c in range(3):
                    cg = cgs[c]
                    kt = c % 2
                    t_ = tmp[kt]
                    tr = f"A:tmp{kt}"
                    self.tt("dve", t_[:], hh[:, c, :], mean[:], ALU.subtract, [f"A:hh{c}", "A:mean"], [tr])
                    self.stt(t_[:], t_[:], vec[:, cg, 5:6], rstd[:], ALU.mult, ALU.mult, [tr, "A:rstd", "mlv"], [tr])
                    self.stt(t_[:], xcl[k2][:, c, :], vec[:, cg, 6:7], t_[:], ALU.mult, ALU.add, [tr, f"A:xcl{k2}_{c}", "mlv"], [tr])
                    self.tt("dve", y_t[:, c, :], t_[:], sgl[k2][:, c, :], ALU.mult, [tr, f"A:sgl{k2}_{c}"], [y_r])
                self.outproj(y_t, y_r, 3, self.wor.tiles[so], self.wor.res[so], tt)
            if hf == 0:
                self.dma("sp", self.scr_ct[j, g], ctf[:], ["A:ctf"], [f"D:ct{j}_{g}"], "sct")

    def finish(self, hf):
        self.phase()
        t0 = hf * TH
        if self.last:
            sq = self.aalloc([128, 8, 512], BF16)
            rs = self.aalloc([128, 512], F32)
            on = self.aalloc([128, 8, TH], F32)
            for tt in range(NT):
                sl = slice(tt * 512, (tt + 1) * 512)
                self.act(sq[:], self.hT[:, :, sl], AF.Square, self.hT_res(range(8), [tt]), ["A:sq"])
                pb = self.ps_next()
                for kc in range(8):
                    self.mm(self.ps[pb][:], self.ones_bf[:], sq[:, kc, :], kc == 0, kc == 7, ["A:sq", "ones_bf"], self.PS(pb))
                self.act(rs[:], self.ps[pb][:], AF.Sqrt, self.PS(pb) + ["epsc"], ["A:rs"], bias=self.epsc[:, 0:1], scale=1.0 / D)
                self.recip(rs[:], rs[:], ["A:rs"], ["A:rs"])
                for kc in range(8):
                    self.stt(on[:, kc, sl], self.hT[:, kc, sl], self.gall[:, 128 + kc:129 + kc], rs[:], ALU.mult, ALU.mult,
                             [f"hT{kc}_{tt}", "A:rs", "gall"], [f"A:on{tt}"])
            self.dma("sp", self.d_out[:, :, t0:t0 + TH], on[:], [f"A:on{tt}" for tt in range(NT)], [], f"out{hf}")
        else:
            self.dma("sp", self.d_out[:, :, t0:t0 + TH], self.hT[:], self.hT_res(), [], f"out{hf}")


_PROG_CACHE = {}


def _get_prog(layers, first, last):
    key = (tuple(layers), first, last)
    if key not in _PROG_CACHE:
        b = Builder(list(layers), first, last)
        nc, cnt = b.build()
        _PROG_CACHE[key] = nc
    return _PROG_CACHE[key]


LAUNCH_PLAN = [[0, 1, 2, 3]]


def run_layers(shared, h_list, mem_list, layers, last, cores=NCORES, trace=False):
    nc = _get_prog(layers, layers[0] == 0, last)
    in_maps = []
    for c in range(cores):
        m = dict(shared)
        m["hin"] = h_list[c]
        m["memT"] = mem_list[c]
        in_maps.append(m)
    res = run_bass_kernel_spmd(nc, in_maps, core_ids=list(range(cores)), trace=trace)
    return [r["hout"] for r in res.results], res


def kernel(**inputs):
    x = np.asarray(inputs["x"], np.float32)
    mem = np.asarray(inputs["mem"], np.float32)
    shared = prep_shared(inputs)
    hs, ms = [], []
    for b in range(NCORES):
        xin, memT = prep_core(x[b], mem[b])
        hs.append(xin)
        ms.append(memT)
    for i, layers in enumerate(LAUNCH_PLAN):
        hs, _ = run_layers(shared, hs, ms, layers, last=(i == len(LAUNCH_PLAN) - 1))
    out = np.empty((NCORES, S, D), np.float32)
    for b in range(NCORES):
        out[b] = hs[b].transpose(2, 1, 0).reshape(S, D)
    return out
```
